# Optimizing a Trainium2 kernel written in Bass

```python
import jax, jax.numpy as jnp
from jax import lax
import numpy as np

D_MODEL = 1024
BATCH = 2
SEQ = 8192
DEPTH = 4

HEAD_DIM = 64
ATTN_HEADS_PER_GROUP = 8
DILATED_GROUPS = ((128, 1), (512, 4), (2048, 16))
N_GROUPS = 3
ATTN_QKV_WIDTH = N_GROUPS * ATTN_HEADS_PER_GROUP * HEAD_DIM
ATTN_OUT_WIDTH = ATTN_HEADS_PER_GROUP * HEAD_DIM
ROPE_THETA = 10000.0
BAND_BLOCK = 128
LRU_WIDTH = D_MODEL
LRU_BLOCKS = 16
LRU_BLOCK_DIM = LRU_WIDTH // LRU_BLOCKS
LRU_CONV_WIDTH = 4
LRU_C = 8.0
CONV_WIDTH = D_MODEL
CONV_KERNEL = 31
MLP_HIDDEN = 4 * D_MODEL
N_BRANCHES = 3
N_IN = 3 * ATTN_QKV_WIDTH + 2 * LRU_WIDTH + 2 * CONV_WIDTH + N_BRANCHES * D_MODEL
IN_SPLITS = (ATTN_QKV_WIDTH, 2 * ATTN_QKV_WIDTH, 3 * ATTN_QKV_WIDTH,
             3 * ATTN_QKV_WIDTH + LRU_WIDTH, 3 * ATTN_QKV_WIDTH + 2 * LRU_WIDTH,
             3 * ATTN_QKV_WIDTH + 2 * LRU_WIDTH + 2 * CONV_WIDTH)
EPS = 1e-6
NEG_INF = -1e30

kernel_name = 'hybrid_dilated_attn_rglru_conformer_gated_trunk'


def rms_norm(x, g):
    xf = x.astype(jnp.float32)
    y = xf * lax.rsqrt(jnp.mean(xf * xf, axis=-1, keepdims=True) + EPS)
    return (y * g.astype(jnp.float32)).astype(x.dtype)


def layer_norm(x, g, b):
    xf = x.astype(jnp.float32)
    mu = jnp.mean(xf, axis=-1, keepdims=True)
    var = jnp.mean(jnp.square(xf - mu), axis=-1, keepdims=True)
    y = (xf - mu) * lax.rsqrt(var + EPS) * g.astype(jnp.float32) + b.astype(jnp.float32)
    return y.astype(x.dtype)


def causal_dwconv(x, w, b):
    K, C = w.shape
    xp = jnp.pad(x, ((0, 0), (K - 1, 0), (0, 0)))
    y = lax.conv_general_dilated(xp, w[:, None, :].astype(x.dtype), window_strides=(1,),
                                 padding='VALID', dimension_numbers=('NWC', 'WIO', 'NWC'),
                                 feature_group_count=C)
    return y + b.astype(x.dtype)


def rope_tables(positions):
    inv_freq = ROPE_THETA ** (-jnp.arange(0, HEAD_DIM, 2, dtype=jnp.float32) / HEAD_DIM)
    ang = positions.astype(jnp.float32)[..., None] * inv_freq
    return jnp.cos(ang)[:, :, None, None, :], jnp.sin(ang)[:, :, None, None, :]


def apply_rope(t, cos, sin):
    tf = t.astype(jnp.float32)
    t1, t2 = jnp.split(tf, 2, axis=-1)
    return jnp.concatenate([t1 * cos - t2 * sin, t2 * cos + t1 * sin], axis=-1)


def banded_causal_attention(q, k, v, win):
    N, L, H, Dh = q.shape
    nb = -(-L // BAND_BLOCK)
    pad = nb * BAND_BLOCK - L
    qb = jnp.pad(q, ((0, 0), (0, pad), (0, 0), (0, 0))).reshape(N, nb, BAND_BLOCK, H, Dh)

    def with_prev(t):
        tb = jnp.pad(t, ((0, 0), (BAND_BLOCK, pad), (0, 0), (0, 0))).reshape(N, nb + 1, BAND_BLOCK, H, Dh)
        return jnp.concatenate([tb[:, :-1], tb[:, 1:]], axis=2)

    kb, vb = with_prev(k), with_prev(v)
    s = jnp.einsum('nbqhd,nbkhd->nbhqk', qb, kb) * (Dh ** -0.5)
    qi = jnp.arange(BAND_BLOCK)[:, None]
    kj = jnp.arange(2 * BAND_BLOCK)[None, :]
    dist = BAND_BLOCK + qi - kj
    key_pos = (jnp.arange(nb)[:, None, None] - 1) * BAND_BLOCK + kj[None]
    valid = (dist >= 0) & (dist <= win) & (key_pos >= 0)
    s = jnp.where(valid[None, :, None], s, NEG_INF)
    m = jnp.max(s, axis=-1, keepdims=True)
    p = jnp.exp(s - m)
    l = jnp.sum(p, axis=-1, keepdims=True)
    o = jnp.einsum('nbhqk,nbkhd->nbqhd', p, vb) / jnp.swapaxes(l, 2, 3)
    lse = jnp.swapaxes((m + jnp.log(l))[..., 0], 2, 3)
    o = o.reshape(N, nb * BAND_BLOCK, H, Dh)[:, :L]
    lse = lse.reshape(N, nb * BAND_BLOCK, H)[:, :L]
    return o, lse


def dilated_group_attention(q, k, v, window, dilation):
    B, S, H, Dh = q.shape
    L = S // dilation

    def to_residue(t):
        return t.reshape(B, L, dilation, H, Dh).transpose(0, 2, 1, 3, 4).reshape(B * dilation, L, H, Dh)

    o, lse = banded_causal_attention(to_residue(q), to_residue(k), to_residue(v), window // dilation)
    o = o.reshape(B, dilation, L, H, Dh).transpose(0, 2, 1, 3, 4).reshape(B, S, H, Dh)
    lse = lse.reshape(B, dilation, L, H).transpose(0, 2, 1, 3).reshape(B, S, H)
    return o, lse


def dilated_attention_mixer(q, k, v, cos, sin, w_o):
    B, S, _ = q.shape
    shape5 = (B, S, N_GROUPS, ATTN_HEADS_PER_GROUP, HEAD_DIM)
    qf = apply_rope(q.reshape(shape5), cos, sin)
    kf = apply_rope(k.reshape(shape5), cos, sin)
    vf = v.reshape(shape5).astype(jnp.float32)
    outs, lses = [], []
    for g, (window, dilation) in enumerate(DILATED_GROUPS):
        o, lse = dilated_group_attention(qf[:, :, g], kf[:, :, g], vf[:, :, g], window, dilation)
        outs.append(o)
        lses.append(lse)
    o = jnp.stack(outs, axis=0)
    wts = jax.nn.softmax(jnp.stack(lses, axis=0), axis=0)[..., None]
    o = jnp.sum(wts * o, axis=0).reshape(B, S, ATTN_OUT_WIDTH)
    return o.astype(q.dtype) @ w_o


def rg_lru_mixer(xb, gate_b, conv_w, conv_b, wa, ba, wx, bx, lam, w_o):
    u = causal_dwconv(xb, conv_w, conv_b)
    B, S, _ = u.shape
    uf = u.astype(jnp.float32)
    ub = uf.reshape(B, S, LRU_BLOCKS, LRU_BLOCK_DIM)
    r = jax.nn.sigmoid(jnp.einsum('bsni,nij->bsnj', ub, wa.astype(jnp.float32)).reshape(B, S, LRU_WIDTH)
                       + ba.astype(jnp.float32))
    i = jax.nn.sigmoid(jnp.einsum('bsni,nij->bsnj', ub, wx.astype(jnp.float32)).reshape(B, S, LRU_WIDTH)
                       + bx.astype(jnp.float32))
    log_a = -LRU_C * r * jax.nn.softplus(-lam.astype(jnp.float32))
    a = jnp.exp(log_a)
    b = jnp.sqrt(-jnp.expm1(2.0 * log_a)) * (i * uf)

    def combine(c1, c2):
        a1, b1 = c1
        a2, b2 = c2
        return a1 * a2, a2 * b1 + b2

    _, h = lax.associative_scan(combine, (a, b), axis=1)
    y = h * jax.nn.gelu(gate_b.astype(jnp.float32))
    return y.astype(xb.dtype) @ w_o


def conformer_conv_mixer(glu_in, dw_w, dw_b, ln_g, ln_b, w_o):
    val, gate = jnp.split(glu_in, 2, axis=-1)
    u = val * jax.nn.sigmoid(gate)
    u = causal_dwconv(u, dw_w, dw_b)
    u = jax.nn.silu(layer_norm(u, ln_g, ln_b))
    return u @ w_o


def setup_inputs(seed: int = 0) -> dict:
    key = jax.random.key(seed)
    ks = jax.random.split(key, 24)
    f32 = jnp.float32

    def nrm(k, shape, fan_in):
        return jax.random.normal(k, shape, f32) * (fan_in ** -0.5)

    def gain(k, shape):
        return 1.0 + 0.02 * jax.random.normal(k, shape, f32)

    def bias(k, shape):
        return 0.02 * jax.random.normal(k, shape, f32)

    u = jax.random.uniform(ks[20], (DEPTH, LRU_WIDTH), f32, 0.9, 0.999)
    a_base = u ** (1.0 / LRU_C)
    lru_lambda = jnp.log(a_base) - jnp.log1p(-a_base)
    return {
        'x': jax.random.normal(ks[0], (BATCH, SEQ, D_MODEL), f32),
        'positions': jnp.broadcast_to(jnp.arange(SEQ, dtype=jnp.int32), (BATCH, SEQ)),
        'norm_mix_pre': gain(ks[1], (DEPTH, D_MODEL)),
        'norm_mix_post': gain(ks[2], (DEPTH, D_MODEL)),
        'norm_mlp_pre': gain(ks[3], (DEPTH, D_MODEL)),
        'norm_mlp_post': gain(ks[4], (DEPTH, D_MODEL)),
        'w_in': nrm(ks[5], (DEPTH, D_MODEL, N_IN), D_MODEL),
        'w_o_attn': nrm(ks[6], (DEPTH, ATTN_OUT_WIDTH, D_MODEL), ATTN_OUT_WIDTH),
        'lru_conv_w': nrm(ks[7], (DEPTH, LRU_CONV_WIDTH, LRU_WIDTH), LRU_CONV_WIDTH),
        'lru_conv_b': bias(ks[8], (DEPTH, LRU_WIDTH)),
        'lru_wa': nrm(ks[9], (DEPTH, LRU_BLOCKS, LRU_BLOCK_DIM, LRU_BLOCK_DIM), LRU_BLOCK_DIM),
        'lru_ba': bias(ks[10], (DEPTH, LRU_WIDTH)),
        'lru_wx': nrm(ks[11], (DEPTH, LRU_BLOCKS, LRU_BLOCK_DIM, LRU_BLOCK_DIM), LRU_BLOCK_DIM),
        'lru_bx': bias(ks[12], (DEPTH, LRU_WIDTH)),
        'lru_lambda': lru_lambda,
        'w_o_lru': nrm(ks[13], (DEPTH, LRU_WIDTH, D_MODEL), LRU_WIDTH),
        'conv_dw_w': nrm(ks[14], (DEPTH, CONV_KERNEL, CONV_WIDTH), CONV_KERNEL),
        'conv_dw_b': bias(ks[15], (DEPTH, CONV_WIDTH)),
        'conv_ln_g': gain(ks[16], (DEPTH, CONV_WIDTH)),
        'conv_ln_b': bias(ks[17], (DEPTH, CONV_WIDTH)),
        'w_o_conv': nrm(ks[18], (DEPTH, CONV_WIDTH, D_MODEL), CONV_WIDTH),
        'w_out': nrm(ks[19], (DEPTH, D_MODEL, D_MODEL), D_MODEL),
        'w_mlp_up': nrm(ks[21], (DEPTH, D_MODEL, MLP_HIDDEN), D_MODEL),
        'w_mlp_down': nrm(ks[22], (DEPTH, MLP_HIDDEN, D_MODEL), MLP_HIDDEN),
    }


def reference(x, positions, norm_mix_pre, norm_mix_post, norm_mlp_pre, norm_mlp_post, w_in, w_o_attn,
              lru_conv_w, lru_conv_b, lru_wa, lru_ba, lru_wx, lru_bx, lru_lambda, w_o_lru,
              conv_dw_w, conv_dw_b, conv_ln_g, conv_ln_b, w_o_conv, w_out, w_mlp_up, w_mlp_down):
    B, S, D = x.shape
    cos, sin = rope_tables(positions)
    for l in range(DEPTH):
        h = rms_norm(x, norm_mix_pre[l])
        proj = h @ w_in[l]
        q, k, v, lru_x, lru_gate, glu_in, gate_logits = jnp.split(proj, IN_SPLITS, axis=-1)
        y_a = dilated_attention_mixer(q, k, v, cos, sin, w_o_attn[l])
        y_b = rg_lru_mixer(lru_x, lru_gate, lru_conv_w[l], lru_conv_b[l], lru_wa[l], lru_ba[l],
                           lru_wx[l], lru_bx[l], lru_lambda[l], w_o_lru[l])
        y_c = conformer_conv_mixer(glu_in, conv_dw_w[l], conv_dw_b[l], conv_ln_g[l], conv_ln_b[l], w_o_conv[l])
        g = jax.nn.sigmoid(gate_logits.astype(jnp.float32)).reshape(B, S, N_BRANCHES, D)
        merged = (g[:, :, 0] * y_a.astype(jnp.float32) + g[:, :, 1] * y_b.astype(jnp.float32)
                  + g[:, :, 2] * y_c.astype(jnp.float32)).astype(x.dtype)
        x = x + rms_norm(merged @ w_out[l], norm_mix_post[l])
        h = rms_norm(x, norm_mlp_pre[l])
        u = jnp.square(jax.nn.relu(h @ w_mlp_up[l]))
        x = x + rms_norm(u @ w_mlp_down[l], norm_mlp_post[l])
    return x
```

```python
import numpy as np
import ml_dtypes
from contextlib import ExitStack
import concourse.bass as bass
import concourse.mybir as mybir
from concourse.bass_utils import run_bass_kernel_spmd

F32 = mybir.dt.float32
BF16 = mybir.dt.bfloat16
I32 = mybir.dt.int32
AF = mybir.ActivationFunctionType
ALU = mybir.AluOpType
NPBF = ml_dtypes.bfloat16

D = 1024
S = 8192
B = 2
NL = 4
TOK = 2048
T = 512
NT = TOK // T
EPS = 1e-6
PI = float(np.pi)


class Buf:
    __slots__ = ("t", "lastw", "readers", "name")

    def __init__(self, t, name=""):
        self.t = t
        self.lastw = None
        self.readers = {}
        self.name = name

    def __getitem__(self, k):
        return self.t[k]


class Eng:
    def __init__(self, name, e, sem, same_sync):
        self.name = name
        self.e = e
        self.sem = sem
        self.cnt = 0
        self.waited = {}
        self.same_sync = same_sync


class KB:
    def __init__(self, n_dma_sems=32):
        self.nc = bass.Bass("TRN2", target_bir_lowering=False)
        self.es = ExitStack()
        nc = self.nc
        self.engs = {}
        for name, e, ss in (("pe", nc.tensor, False), ("act", nc.scalar, True), ("dve", nc.vector, True),
                            ("pool", nc.gpsimd, True), ("sp", nc.sync, True)):
            sem = self.es.enter_context(nc.semaphore("s_" + name))
            self.engs[name] = Eng(name, e, sem, ss)
        self.dsems = []
        for i in range(n_dma_sems):
            sem = self.es.enter_context(nc.semaphore(f"s_dma{i}"))
            self.dsems.append([sem, 0])
        self.drr = 0
        self.semkeys = {}
        self.nbuf = 0
        self.psum_rr = 0
        self.psums = []
        self.out_tokens = []

    def sb(self, shape, dt, name=None, es=None):
        self.nbuf += 1
        name = name or f"b{self.nbuf}"
        t = (es or self.es).enter_context(self.nc.sbuf_tensor(f"{name}_{self.nbuf}", list(shape), dt))
        return Buf(t, name)

    def ps(self, shape, dt, name=None, es=None):
        self.nbuf += 1
        name = name or f"p{self.nbuf}"
        t = (es or self.es).enter_context(self.nc.psum_tensor(f"{name}_{self.nbuf}", list(shape), dt))
        return Buf(t, name)

    def dram(self, name, shape, dt, kind):
        t = self.nc.dram_tensor(name, list(shape), dt, kind=kind)
        b = Buf(t.ap(), name)
        return b

    def alloc_psum_banks(self):
        self.psums = [self.ps([128, 512], F32, f"bank{i}") for i in range(8)]

    def bank(self):
        b = self.psums[self.psum_rr]
        self.psum_rr = (self.psum_rr + 1) % 8
        return b

    def _wait(self, eng, tok):
        key, sem, val = tok
        if eng.waited.get(key, 0) >= val:
            return
        eng.e.wait_ge(sem, val)
        eng.waited[key] = val

    def _collect(self, reads, writes):
        need = {}

        def add(tok):
            if tok is None:
                return
            k = tok[0]
            if k not in need or need[k][2] < tok[2]:
                need[k] = tok

        for b in reads:
            add(b.lastw)
        for b in writes:
            add(b.lastw)
            for tok in b.readers.values():
                add(tok)
        return need

    def _mark(self, tok, reads, writes):
        for b in reads:
            b.readers[tok[0]] = tok
        for b in writes:
            b.lastw = tok
            b.readers = {}

    def op(self, ename, fn, reads=(), writes=(), inc=True):
        eng = self.engs[ename]
        need = self._collect(reads, writes)
        for k, tok in need.items():
            if k == eng.name and not eng.same_sync:
                continue
            self._wait(eng, tok)
        ins = fn(eng.e)
        if inc:
            ins.then_inc(eng.sem, 1)
            eng.cnt += 1
            tok = (eng.name, eng.sem, eng.cnt)
        else:
            tok = (eng.name, eng.sem, eng.cnt + 1)
        self._mark(tok, reads, writes)
        return tok

    def dma(self, q, out_ap, in_ap, reads=(), writes=()):
        eng = self.engs[q]
        need = self._collect(reads, writes)
        idx = self.drr
        self.drr = (self.drr + 1) % len(self.dsems)
        ds = self.dsems[idx]
        key = f"dma{idx}"
        if ds[1] > 0:
            need_prev = (key, ds[0], ds[1])
            if key not in need or need[key][2] < ds[1]:
                need[key] = need_prev
        for k, tok in need.items():
            self._wait(eng, tok)
        ds[1] += 16
        eng.e.dma_start(out=out_ap, in_=in_ap).then_inc(ds[0], 16)
        tok = (key, ds[0], ds[1])
        self._mark(tok, reads, writes)
        return tok

    def barrier(self):
        toks = []
        for e in self.engs.values():
            if e.cnt > 0:
                toks.append((e.name, e.sem, e.cnt))
        for i, ds in enumerate(self.dsems):
            if ds[1] > 0:
                toks.append((f"dma{i}", ds[0], ds[1]))
        for e in self.engs.values():
            for tok in toks:
                if tok[0] == e.name:
                    continue
                self._wait(e, tok)

    def finish(self):
        eng = self.engs["sp"]
        for i, ds in enumerate(self.dsems):
            if ds[1] > 0:
                self._wait(eng, (f"dma{i}", ds[0], ds[1]))
        self.es.close()
        return self.nc


class WStream:
    def __init__(self, kb, specs, nbuf, ahead, es=None, queue="pool"):
        self.kb = kb
        self.specs = specs
        self.nbuf = nbuf
        self.ahead = ahead
        self.queue = queue
        self.bufs = [kb.sb([128, 4096], BF16, f"ws{i}", es) for i in range(nbuf)]
        self.issued = 0
        self.released = 0
        self.cur = 0
        self.views = {}

    def _issue_upto(self, k):
        k = min(k, len(self.specs) - 1)
        while self.issued <= k:
            i = self.issued
            if i - self.nbuf >= self.released:
                break
            src, kc, ncols = self.specs[i]
            w_ = self.bufs[i % self.nbuf]
            view = w_[:, 0:kc * ncols].rearrange("p (k n) -> p k n", k=kc)
            self.kb.dma(self.queue, view, src, writes=[w_])
            self.views[i] = (w_, view)
            self.issued += 1

    def next(self):
        i = self.cur
        self.cur += 1
        self._issue_upto(i + self.ahead)
        assert i in self.views, (i, self.issued, self.released)
        w_, view = self.views.pop(i)
        return w_, view

    def release(self, n=1):
        self.released += n
        self._issue_upto(self.cur - 1 + self.ahead)


NA = 11776


def build_LA():
    kb = KB()
    nc = kb.nc
    xT = kb.dram("xT", [128, 8, TOK], F32, "ExternalInput")
    pos = kb.dram("pos", [128, TOK], I32, "ExternalInput")
    crope = kb.dram("crope", [128, 2], F32, "ExternalInput")
    gpre = kb.dram("gpre", [128, 8], F32, "ExternalInput")
    wA = kb.dram("wA", [128, 8, NA], F32, "ExternalInput")
    qT = kb.dram("qT", [128, 12, TOK], BF16, "ExternalOutput")
    kT = kb.dram("kT", [128, 12, TOK], BF16, "ExternalOutput")
    vT = kb.dram("vT", [128, 12, TOK], BF16, "ExternalOutput")
    lxT = kb.dram("lxT", [128, 8, TOK], F32, "ExternalOutput")
    gelT = kb.dram("gelT", [128, 8, TOK], F32, "ExternalOutput")
    ugT = kb.dram("ugT", [128, 8, TOK], F32, "ExternalOutput")
    hTo = kb.dram("hT", [128, 8, TOK], BF16, "ExternalOutput")

    kb.alloc_psum_banks()
    ones = kb.sb([128, 128], F32, "ones")
    kb.op("pool", lambda e: e.memset(ones[:], 1.0), writes=[ones])
    cr = kb.sb([128, 2], F32, "crope")
    kb.dma("sp", cr[:], crope[:], writes=[cr])
    gp = kb.sb([128, 8], F32, "gpre")
    kb.dma("sp", gp[:], gpre[:], writes=[gp])

    cosF = kb.sb([128, TOK], F32, "cosF")
    sinS = kb.sb([128, TOK], F32, "sinS")
    es2 = ExitStack()
    posi = kb.sb([128, TOK], I32, "posi", es2)
    ang = kb.sb([128, TOK], F32, "ang", es2)
    tmpa = kb.sb([128, TOK], F32, "tmpa", es2)
    kb.dma("sp", posi[:], pos[:], writes=[posi])
    kb.op("dve", lambda e: e.tensor_copy(out=ang[:], in_=posi[:]), reads=[posi], writes=[ang])
    kb.op("dve", lambda e: e.tensor_scalar(out=ang[:], in0=ang[:], scalar1=cr[:, 0:1], scalar2=None, op0=ALU.mult),
          reads=[ang, cr], writes=[ang])
    tq = kb.sb([128, TOK], F32, "tq", es2)
    tii = kb.sb([128, TOK], I32, "tii", es2)
    gg_ = kb.sb([128, TOK], F32, "gg", es2)

    def sin_turns(dst, shift):
        kb.op("dve", lambda e: e.tensor_scalar(out=tq[:], in0=ang[:], scalar1=1.0 / (2 * PI), scalar2=shift, op0=ALU.mult, op1=ALU.add),
              reads=[ang], writes=[tq])
        kb.op("dve", lambda e: e.tensor_copy(out=tii[:], in_=tq[:]), reads=[tq], writes=[tii])
        kb.op("dve", lambda e: e.tensor_copy(out=tmpa[:], in_=tii[:]), reads=[tii], writes=[tmpa])
        kb.op("dve", lambda e: e.tensor_tensor(out=tq[:], in0=tq[:], in1=tmpa[:], op=ALU.subtract), reads=[tq, tmpa], writes=[tq])
        kb.op("dve", lambda e: e.tensor_scalar(out=gg_[:], in0=tq[:], scalar1=0.5, scalar2=None, op0=ALU.is_gt), reads=[tq], writes=[gg_])
        kb.op("dve", lambda e: e.tensor_tensor(out=tq[:], in0=tq[:], in1=gg_[:], op=ALU.subtract), reads=[tq, gg_], writes=[tq])
        kb.op("dve", lambda e: e.tensor_scalar(out=gg_[:], in0=tq[:], scalar1=-0.5, scalar2=None, op0=ALU.is_lt), reads=[tq], writes=[gg_])
        kb.op("dve", lambda e: e.tensor_tensor(out=tq[:], in0=tq[:], in1=gg_[:], op=ALU.add), reads=[tq, gg_], writes=[tq])
        kb.op("act", lambda e: e.activation(out=dst[:], in_=tq[:], func=AF.Sin, scale=2 * PI), reads=[tq], writes=[dst])

    sin_turns(tmpa, 0.0)
    kb.op("dve", lambda e: e.tensor_scalar(out=sinS[:], in0=tmpa[:], scalar1=cr[:, 1:2], scalar2=None, op0=ALU.mult),
          reads=[tmpa, cr], writes=[sinS])
    sin_turns(cosF, 0.25)
    kb.barrier()
    es2.close()

    hT = [kb.sb([128, 8, T], BF16, f"hT{t}") for t in range(NT)]
    es3 = ExitStack()
    xpool = [kb.sb([128, 8, T], F32, f"x{i}", es3) for i in range(2)]
    sqpool = [kb.sb([128, T], F32, f"sq{i}", es3) for i in range(3)]
    sd = kb.sb([128, T], F32, "sd", es3)
    rstd = kb.sb([128, T], F32, "rstd", es3)
    for t in range(NT):
        xb = xpool[t % 2]
        kb.dma("sp", xb[:], xT[:, :, t * T:(t + 1) * T], writes=[xb])
        ps = kb.bank()
        for c in range(8):
            sq = sqpool[c % 3]
            kb.op("act", lambda e, c=c, sq=sq: e.activation(out=sq[:], in_=xb[:, c, :], func=AF.Square),
                  reads=[xb], writes=[sq])
            kb.op("pe", lambda e, c=c, sq=sq: e.matmul(ps[:], ones[:], sq[:], start=(c == 0), stop=(c == 7)),
                  reads=[ones, sq], writes=[ps])
        kb.op("dve", lambda e: e.tensor_scalar(out=sd[:], in0=ps[:], scalar1=1.0 / D, scalar2=EPS, op0=ALU.mult, op1=ALU.add),
              reads=[ps], writes=[sd])
        kb.op("act", lambda e: e.activation(out=sd[:], in_=sd[:], func=AF.Sqrt), reads=[sd], writes=[sd])
        kb.op("dve", lambda e: e.reciprocal(out=rstd[:], in_=sd[:]), reads=[sd], writes=[rstd])
        for c in range(8):
            kb.op("dve", lambda e, c=c: e.scalar_tensor_tensor(out=hT[t][:, c, :], in0=xb[:, c, :], scalar=gp[:, c:c + 1],
                                                            in1=rstd[:], op0=ALU.mult, op1=ALU.mult),
                  reads=[xb, gp, rstd], writes=[hT[t]])
        kb.dma("sp", hTo[:, :, t * T:(t + 1) * T], hT[t][:], reads=[hT[t]])
    kb.barrier()
    es3.close()

    nblk = NA // 512
    wsA = WStream(kb, [(wA[:, :, blk * 512:(blk + 1) * 512], 8, 512) for blk in range(nblk)], 4, 2)
    obf = [kb.sb([128, T], BF16, f"obf{i}") for i in range(4)]
    of32 = [kb.sb([128, T], F32, f"of{i}") for i in range(4)]
    tmp1 = [kb.sb([128, T], F32, f"t1_{i}") for i in range(3)]
    tmp2 = [kb.sb([128, T], F32, f"t2_{i}") for i in range(3)]
    cnt = {"obf": 0, "of": 0, "tmp": 0}

    def mm(wb_unused, col0, t):
        ps = kb.bank()
        for kc in range(8):
            kb.op("pe", lambda e, kc=kc: e.matmul(ps[:], wb[:, kc, col0:col0 + 128], hT[t][:, kc, :], start=(kc == 0), stop=(kc == 7)),
                  reads=[wbuf_, hT[t]], writes=[ps], inc=(kc == 7))
        return ps

    for blk in range(nblk):
        if blk > 0:
            wsA.release(1)
        wbuf_, wb = wsA.next()
        if blk < 12:
            for u in range(2):
                unit = blk * 2 + u
                dst = qT if unit < 12 else kT
                j = unit % 12
                for t in range(NT):
                    psA = mm(wb, u * 256, t)
                    psB = mm(wb, u * 256 + 128, t)
                    i = cnt["tmp"]; cnt["tmp"] += 1
                    m1 = tmp1[i % 3]; m2 = tmp2[i % 3]
                    kb.op("dve", lambda e: e.tensor_tensor(out=m1[:], in0=psA[:], in1=cosF[:, t * T:(t + 1) * T], op=ALU.mult),
                          reads=[psA, cosF], writes=[m1])
                    kb.op("dve", lambda e: e.tensor_tensor(out=m2[:], in0=psB[:], in1=sinS[:, t * T:(t + 1) * T], op=ALU.mult),
                          reads=[psB, sinS], writes=[m2])
                    ob = obf[cnt["obf"] % 4]; cnt["obf"] += 1
                    kb.op("pool", lambda e: e.tensor_tensor(out=ob[:], in0=m1[:], in1=m2[:], op=ALU.add),
                          reads=[m1, m2], writes=[ob])
                    kb.dma("sp", dst[:, j, t * T:(t + 1) * T], ob[:], reads=[ob])
        elif blk < 15:
            for u in range(4):
                j = (blk - 12) * 4 + u
                for t in range(NT):
                    ps = mm(wb, u * 128, t)
                    ob = obf[cnt["obf"] % 4]; cnt["obf"] += 1
                    kb.op("act", lambda e: e.activation(out=ob[:], in_=ps[:], func=AF.Copy), reads=[ps], writes=[ob])
                    kb.dma("sp", vT[:, j, t * T:(t + 1) * T], ob[:], reads=[ob])
        elif blk < 19:
            for u in range(4):
                j = ((blk - 15) % 2) * 4 + u
                isx = blk < 17
                dst = lxT if isx else gelT
                for t in range(NT):
                    ps = mm(wb, u * 128, t)
                    ob = of32[cnt["of"] % 4]; cnt["of"] += 1
                    kb.op("act", lambda e: e.activation(out=ob[:], in_=ps[:], func=(AF.Copy if isx else AF.Gelu)),
                          reads=[ps], writes=[ob])
                    kb.dma("sp", dst[:, j, t * T:(t + 1) * T], ob[:], reads=[ob])
        else:
            for u in range(2):
                j = (blk - 19) * 2 + u
                for t in range(NT):
                    psA = mm(wb, u * 256, t)
                    psB = mm(wb, u * 256 + 128, t)
                    i = cnt["tmp"]; cnt["tmp"] += 1
                    sg = tmp1[i % 3]
                    kb.op("act", lambda e: e.activation(out=sg[:], in_=psB[:], func=AF.Sigmoid), reads=[psB], writes=[sg])
                    ob = of32[cnt["of"] % 4]; cnt["of"] += 1
                    kb.op("dve", lambda e: e.tensor_tensor(out=ob[:], in0=psA[:], in1=sg[:], op=ALU.mult),
                          reads=[psA, sg], writes=[ob])
                    kb.dma("sp", ugT[:, j, t * T:(t + 1) * T], ob[:], reads=[ob])
    return kb.finish()


DIL = (1, 4, 16)
TL = 2048


def build_LB():
    kb = KB()
    qa = kb.dram("qa", [64, 6, S], BF16, "ExternalInput")
    ka = kb.dram("ka", [64, 6, S], BF16, "ExternalInput")
    va = kb.dram("va", [128, 6, 64, 64], BF16, "ExternalInput")
    cmask = kb.dram("cmask", [128, 256], BF16, "ExternalInput")
    lx = kb.dram("lx", [128, 2, S], F32, "ExternalInput")
    gel = kb.dram("gel", [128, 2, S], F32, "ExternalInput")
    ug = kb.dram("ug", [128, 2, S], F32, "ExternalInput")
    lp = kb.dram("lp", [128, 8], F32, "ExternalInput")
    wab = kb.dram("wab", [128, 256], F32, "ExternalInput")
    cw = kb.dram("cw", [128, 32], F32, "ExternalInput")
    oT = kb.dram("oT", [64, 2, S], BF16, "ExternalOutput")
    yb = kb.dram("yb", [128, 2, S], BF16, "ExternalOutput")
    cv = kb.dram("cv", [128, 2, S], F32, "ExternalOutput")

    kb.alloc_psum_banks()
    onesb = kb.sb([128, 64], BF16, "onesb")
    kb.op("pool", lambda e: e.memset(onesb[:], 1.0), writes=[onesb])
    one1 = kb.sb([128, 1], F32, "one1")
    kb.op("pool", lambda e: e.memset(one1[:], 1.0), writes=[one1])
    zero1 = kb.sb([128, 1], F32, "zero1")
    kb.op("pool", lambda e: e.memset(zero1[:], 0.0), writes=[zero1])
    msk = kb.sb([128, 256], BF16, "msk")
    kb.dma("sp", msk[:], cmask[:], writes=[msk])
    lps = kb.sb([128, 8], F32, "lps")
    kb.dma("sp", lps[:], lp[:], writes=[lps])
    cws = kb.sb([128, 32], F32, "cws")
    kb.dma("sp", cws[:], cw[:], writes=[cws])
    wabs = kb.sb([128, 256], BF16, "wabs")
    kb.dma("pool", wabs[:], wab[:], writes=[wabs])

    esA = ExitStack()
    Oacc = kb.sb([64, S], F32, "Oacc", esA)
    Lacc = kb.sb([64, S], F32, "Lacc", esA)
    qs = [kb.sb([64, S], BF16, f"qs{i}", esA) for i in range(2)]
    ks = [kb.sb([64, S], BF16, f"ks{i}", esA) for i in range(2)]
    vs = [kb.sb([128, 64, 64], BF16, f"vs{i}", esA) for i in range(2)]
    pts = [kb.sb([128, 256], BF16, f"pt{i}", esA) for i in range(4)]
    ofin = [kb.sb([64, 2048], BF16, f"ofin{i}", esA) for i in range(2)]
    rl = kb.sb([64, 2048], F32, "rl", esA)
    for bg in range(6):
        b, g = divmod(bg, 3)
        d = DIL[g]
        nb = 64 // d
        q_sb, k_sb, v_sb = qs[bg % 2], ks[bg % 2], vs[bg % 2]
        kb.dma("sp", q_sb[:], qa[:, bg, :], writes=[q_sb])
        kb.dma("sp", k_sb[:], ka[:, bg, :], writes=[k_sb])
        kb.dma("sp", v_sb[:], va[:, bg, :, :], writes=[v_sb])
        Ov = Oacc[:, :].rearrange("p (l d) -> p d l", d=d)
        Lv = Lacc[:, :].rearrange("p (l d) -> p d l", d=d)
        prev_pt = None
        for i in range(64):
            r, j = divmod(i, nb)
            nq = 256 if j != nb - 1 else 128
            sT = kb.bank()
            kb.op("pe", lambda e: e.matmul(sT[:, :nq], k_sb[:, i * 128:(i + 1) * 128], q_sb[:, i * 128:i * 128 + nq], start=True, stop=True),
                  reads=[k_sb, q_sb], writes=[sT])
            pt = pts[i % 4]
            kb.op("act", lambda e: e.activation(out=pt[:, :nq], in_=sT[:, :nq], func=AF.Exp, scale=0.125), reads=[sT], writes=[pt])
            kb.op("pool", lambda e: e.tensor_tensor(out=pt[:, :nq], in0=pt[:, :nq], in1=msk[:, :nq], op=ALU.mult), reads=[pt, msk], writes=[pt])
            o_ps = kb.bank()
            l_ps = kb.bank()
            if j > 0:
                kb.op("pe", lambda e: e.matmul(o_ps[:64, :128], v_sb[:, i - 1, :], prev_pt[:, 128:256], start=True, stop=False),
                      reads=[v_sb, prev_pt], writes=[o_ps], inc=False)
                kb.op("pe", lambda e: e.matmul(l_ps[:64, :128], onesb[:, :], prev_pt[:, 128:256], start=True, stop=False),
                      reads=[onesb, prev_pt], writes=[l_ps], inc=False)
            kb.op("pe", lambda e: e.matmul(o_ps[:64, :128], v_sb[:, i, :], pt[:, 0:128], start=(j == 0), stop=True),
                  reads=[v_sb, pt], writes=[o_ps])
            kb.op("pe", lambda e: e.matmul(l_ps[:64, :128], onesb[:, :], pt[:, 0:128], start=(j == 0), stop=True),
                  reads=[onesb, pt], writes=[l_ps])
            od = Ov[:, r, j * 128:(j + 1) * 128]
            ld = Lv[:, r, j * 128:(j + 1) * 128]
            if g == 0:
                kb.op("act", lambda e: e.activation(out=od, in_=o_ps[:64, :128], func=AF.Copy), reads=[o_ps], writes=[Oacc])
                kb.op("dve", lambda e: e.tensor_copy(out=ld, in_=l_ps[:64, :128]), reads=[l_ps], writes=[Lacc])
            else:
                kb.op("dve", lambda e: e.tensor_tensor(out=od, in0=o_ps[:64, :128], in1=od, op=ALU.add), reads=[o_ps, Oacc], writes=[Oacc])
                kb.op("dve", lambda e: e.tensor_tensor(out=ld, in0=l_ps[:64, :128], in1=ld, op=ALU.add), reads=[l_ps, Lacc], writes=[Lacc])
            prev_pt = pt
        if g == 2:
            for piece in range(4):
                sl = slice(piece * 2048, (piece + 1) * 2048)
                of_ = ofin[piece % 2]
                kb.op("dve", lambda e: e.reciprocal(out=rl[:], in_=Lacc[:, sl]), reads=[Lacc], writes=[rl])
                kb.op("dve", lambda e: e.tensor_tensor(out=of_[:], in0=Oacc[:, sl], in1=rl[:], op=ALU.mult), reads=[Oacc, rl], writes=[of_])
                kb.dma("sp", oT[:, b, sl], of_[:], reads=[of_])
    kb.barrier()
    esA.close()

    esL = ExitStack()
    ca = kb.sb([128, 1], F32, "ca", esL)
    kb.op("act", lambda e: e.activation(out=ca[:], in_=lps[:, 7:8], func=AF.Exp, scale=-1.0), reads=[lps], writes=[ca])
    kb.op("act", lambda e: e.activation(out=ca[:], in_=ca[:], func=AF.Ln, bias=one1[:, 0:1], scale=1.0), reads=[ca, one1], writes=[ca])
    kb.op("dve", lambda e: e.tensor_scalar(out=ca[:], in0=ca[:], scalar1=-8.0, scalar2=None, op0=ALU.mult), reads=[ca], writes=[ca])
    X = [kb.sb([128, 3 + TL], F32, f"X{i}", esL) for i in range(2)]
    H = [kb.sb([128, TL], F32, f"H{i}", esL) for i in range(2)]
    U = kb.sb([128, TL], F32, "U", esL)
    UB = kb.sb([128, TL], BF16, "UB", esL)
    Rr = kb.sb([128, TL], F32, "Rr", esL)
    Ii = kb.sb([128, TL], F32, "Ii", esL)
    Aa = kb.sb([128, TL], F32, "Aa", esL)
    Ss = kb.sb([128, TL], F32, "Ss", esL)
    Gg = kb.sb([128, TL], F32, "Gg", esL)
    Yy = [kb.sb([128, TL], BF16, f"Yy{i}", esL) for i in range(2)]
    nt = S // TL
    for ti in range(2 * nt):
        b, tt = divmod(ti, nt)
        sl = slice(tt * TL, (tt + 1) * TL)
        Xc, Xp = X[ti % 2], X[(ti + 1) % 2]
        Hc, Hp = H[ti % 2], H[(ti + 1) % 2]
        if tt == 0:
            kb.op("pool", lambda e: e.memset(Xc[:, 0:3], 0.0), writes=[Xc])
        else:
            kb.op("pool", lambda e: e.tensor_copy(out=Xc[:, 0:3], in_=Xp[:, TL:TL + 3]), reads=[Xp], writes=[Xc])
        kb.dma("sp", Xc[:, 3:3 + TL], lx[:, b, sl], writes=[Xc])
        kb.dma("sp", Gg[:], gel[:, b, sl], writes=[Gg])
        kb.op("dve", lambda e: e.tensor_scalar(out=U[:], in0=Xc[:, 0:TL], scalar1=lps[:, 0:1], scalar2=lps[:, 4:5], op0=ALU.mult, op1=ALU.add),
              reads=[Xc, lps], writes=[U])
        for k in range(1, 4):
            kb.op("dve", lambda e, k=k: e.scalar_tensor_tensor(out=U[:], in0=Xc[:, k:k + TL], scalar=lps[:, k:k + 1], in1=U[:], op0=ALU.mult, op1=ALU.add),
                  reads=[Xc, lps, U], writes=[U])
        kb.op("pool", lambda e: e.tensor_copy(out=UB[:], in_=U[:]), reads=[U], writes=[UB])
        for s4 in range(TL // 512):
            ss = slice(s4 * 512, (s4 + 1) * 512)
            psr = kb.bank()
            kb.op("pe", lambda e: e.matmul(psr[:], wabs[:, 0:128], UB[:, ss], start=True, stop=True), reads=[wabs, UB], writes=[psr])
            psi = kb.bank()
            kb.op("pe", lambda e: e.matmul(psi[:], wabs[:, 128:256], UB[:, ss], start=True, stop=True), reads=[wabs, UB], writes=[psi])
            kb.op("act", lambda e: e.activation(out=Rr[:, ss], in_=psr[:], func=AF.Sigmoid, bias=lps[:, 5:6], scale=1.0), reads=[psr, lps], writes=[Rr])
            kb.op("act", lambda e: e.activation(out=Ii[:, ss], in_=psi[:], func=AF.Sigmoid, bias=lps[:, 6:7], scale=1.0), reads=[psi, lps], writes=[Ii])
        kb.op("act", lambda e: e.activation(out=Aa[:], in_=Rr[:], func=AF.Exp, scale=ca[:, 0:1]), reads=[Rr, ca], writes=[Aa])
        kb.op("pool", lambda e: e.tensor_tensor(out=Ss[:], in0=Aa[:], in1=Aa[:], op=ALU.mult), reads=[Aa], writes=[Ss])
        kb.op("act", lambda e: e.activation(out=Ss[:], in_=Ss[:], func=AF.Sqrt, bias=one1[:, 0:1], scale=-1.0), reads=[Ss, one1], writes=[Ss])
        kb.op("pool", lambda e: e.tensor_tensor(out=Ss[:], in0=Ss[:], in1=Ii[:], op=ALU.mult), reads=[Ss, Ii], writes=[Ss])
        kb.op("pool", lambda e: e.tensor_tensor(out=Ss[:], in0=Ss[:], in1=U[:], op=ALU.mult), reads=[Ss, U], writes=[Ss])
        init = zero1[:, 0:1] if tt == 0 else Hp[:, TL - 1:TL]
        kb.op("dve", lambda e: e.tensor_tensor_scan(out=Hc[:], data0=Aa[:], data1=Ss[:], initial=init, op0=ALU.mult, op1=ALU.add),
              reads=[Aa, Ss, Hp, zero1], writes=[Hc])
        Yc = Yy[ti % 2]
        kb.op("pool", lambda e: e.tensor_tensor(out=Yc[:], in0=Hc[:], in1=Gg[:], op=ALU.mult), reads=[Hc, Gg], writes=[Yc])
        kb.dma("sp", yb[:, b, sl], Yc[:], reads=[Yc])
    kb.barrier()
    esL.close()

    esC = ExitStack()
    X2 = [kb.sb([128, 30 + TL], F32, f"X2{i}", esC) for i in range(2)]
    ACC = [kb.sb([128, TL], F32, f"ACC{i}", esC) for i in range(2)]
    for ti in range(2 * nt):
        b, tt = divmod(ti, nt)
        sl = slice(tt * TL, (tt + 1) * TL)
        Xc, Xp = X2[ti % 2], X2[(ti + 1) % 2]
        acc = ACC[ti % 2]
        if tt == 0:
            kb.op("pool", lambda e: e.memset(Xc[:, 0:30], 0.0), writes=[Xc])
        else:
            kb.op("pool", lambda e: e.tensor_copy(out=Xc[:, 0:30], in_=Xp[:, TL:TL + 30]), reads=[Xp], writes=[Xc])
        kb.dma("sp", Xc[:, 30:30 + TL], ug[:, b, sl], writes=[Xc])
        kb.op("dve", lambda e: e.tensor_scalar(out=acc[:], in0=Xc[:, 0:TL], scalar1=cws[:, 0:1], scalar2=cws[:, 31:32], op0=ALU.mult, op1=ALU.add),
              reads=[Xc, cws], writes=[acc])
        for k in range(1, 31):
            kb.op("dve", lambda e, k=k: e.scalar_tensor_tensor(out=acc[:], in0=Xc[:, k:k + TL], scalar=cws[:, k:k + 1], in1=acc[:], op0=ALU.mult, op1=ALU.add),
                  reads=[Xc, cws, acc], writes=[acc])
        kb.dma("sp", cv[:, b, sl], acc[:], reads=[acc])
    kb.barrier()
    esC.close()
    return kb.finish()


def build_LC():
    kb = KB()
    xT = kb.dram("xT", [128, 8, TOK], F32, "ExternalInput")
    hTi = kb.dram("hT", [128, 8, TOK], BF16, "ExternalInput")
    oTi = kb.dram("oT", [128, 4, TOK], BF16, "ExternalInput")
    ybi = kb.dram("ybT", [128, 8, TOK], BF16, "ExternalInput")
    cvi = kb.dram("cvT", [128, 8, TOK], F32, "ExternalInput")
    pv = kb.dram("pv", [128, 5, 8], F32, "ExternalInput")
    wg = kb.dram("wg", [128, 8, 3072], F32, "ExternalInput")
    woa = kb.dram("woa", [128, 4, 1024], F32, "ExternalInput")
    wol = kb.dram("wol", [128, 8, 1024], F32, "ExternalInput")
    woc = kb.dram("woc", [128, 8, 1024], F32, "ExternalInput")
    wout = kb.dram("wout", [128, 8, 1024], F32, "ExternalInput")
    wup = kb.dram("wup", [128, 8, 4096], F32, "ExternalInput")
    wdn = kb.dram("wdn", [128, 32, 1024], F32, "ExternalInput")
    xo = kb.dram("xo", [128, 8, TOK], F32, "ExternalOutput")

    banks = [kb.ps([128, 512], F32, f"bank{i}") for i in range(8)]
    st1, st2 = banks[6], banks[7]
    rr = {"b": 0, "w": 0, "t": 0}

    def bank():
        b_ = banks[rr["b"] % 6]
        rr["b"] += 1
        return b_

    ones = kb.sb([128, 128], F32, "ones")
    kb.op("pool", lambda e: e.memset(ones[:], 1.0), writes=[ones])
    pvs = kb.sb([128, 5, 8], F32, "pvs")
    kb.dma("sp", pvs[:], pv[:], writes=[pvs])
    LNG, LNB, GPOST, GMPRE, GMPOST = range(5)

    xs = kb.sb([128, 8, T], F32, "xs")
    hs = kb.sb([128, 8, T], BF16, "hs")
    os_ = kb.sb([128, 4, T], BF16, "os")
    ybs = kb.sb([128, 8, T], BF16, "ybs")
    cvs = kb.sb([128, 8, T], F32, "cvs")
    ycs = kb.sb([128, 8, T], BF16, "ycs")
    mg = kb.sb([128, 8, T], BF16, "mg")
    u2 = kb.sb([128, 32, T], BF16, "u2")
    specs = []
    for t_ in range(NT):
        for nblk in range(4):
            cs_ = slice(nblk * 256, (nblk + 1) * 256)
            specs.append((woa[:, :, cs_], 4, 256))
            specs.append((wol[:, :, cs_], 8, 256))
            specs.append((woc[:, :, cs_], 8, 256))
            for br in range(3):
                specs.append((wg[:, :, br * 1024 + nblk * 256: br * 1024 + (nblk + 1) * 256], 8, 256))
        for nb2 in range(2):
            specs.append((wout[:, :, nb2 * 512:(nb2 + 1) * 512], 8, 512))
        for ub in range(8):
            specs.append((wup[:, :, ub * 512:(ub + 1) * 512], 8, 512))
        for n_ in range(8):
            specs.append((wdn[:, :, n_ * 128:(n_ + 1) * 128], 32, 128))
    wsC = WStream(kb, specs, 10, 6)
    tp = [kb.sb([128, T], F32, f"tp{i}") for i in range(6)]
    mean = kb.sb([128, T], F32, "mean")
    rstd = kb.sb([128, T], F32, "rstd")
    macc = kb.sb([128, T], F32, "macc")

    def tmp():
        t_ = tp[rr["t"] % 6]
        rr["t"] += 1
        return t_

    def wload(src_ap, kc, ncols):
        i = wsC.cur
        assert wsC.specs[i][1] == kc and wsC.specs[i][2] == ncols
        return wsC.next()

    def mmgroup(ps, wbuf, wview, col0, rhs_buf, kc):
        for k in range(kc):
            kb.op("pe", lambda e, k=k: e.matmul(ps[:], wview[:, k, col0:col0 + 128], rhs_buf[:, k, :], start=(k == 0), stop=(k == kc - 1)),
                  reads=[wbuf, rhs_buf], writes=[ps], inc=(k == kc - 1))

    def rstd_from(stb, dst):
        kb.op("dve", lambda e: e.tensor_scalar(out=dst[:], in0=stb[:], scalar1=1.0 / D, scalar2=EPS, op0=ALU.mult, op1=ALU.add),
              reads=[stb], writes=[dst])
        kb.op("act", lambda e: e.activation(out=dst[:], in_=dst[:], func=AF.Sqrt), reads=[dst], writes=[dst])
        kb.op("dve", lambda e: e.reciprocal(out=dst[:], in_=dst[:]), reads=[dst], writes=[dst])

    def stat_acc(stb, src_ap, src_buf, n, square=True):
        if square:
            sq = tmp()
            kb.op("act", lambda e: e.activation(out=sq[:], in_=src_ap, func=AF.Square), reads=[src_buf], writes=[sq])
            kb.op("pe", lambda e: e.matmul(stb[:], ones[:], sq[:], start=(n == 0), stop=(n == 7)), reads=[ones, sq], writes=[stb])
        else:
            kb.op("pe", lambda e: e.matmul(stb[:], ones[:], src_ap, start=(n == 0), stop=(n == 7)), reads=[ones, src_buf], writes=[stb])

    for t in range(NT):
        ts = slice(t * T, (t + 1) * T)
        kb.dma("sp", xs[:], xT[:, :, ts], writes=[xs])
        kb.dma("sp", hs[:], hTi[:, :, ts], writes=[hs])
        kb.dma("sp", os_[:], oTi[:, :, ts], writes=[os_])
        kb.dma("sp", ybs[:], ybi[:, :, ts], writes=[ybs])
        kb.dma("sp", cvs[:], cvi[:, :, ts], writes=[cvs])
        for c in range(8):
            stat_acc(st1, cvs[:, c, :], cvs, c, square=False)
            stat_acc(st2, cvs[:, c, :], cvs, c, square=True)
        kb.op("dve", lambda e: e.tensor_scalar(out=mean[:], in0=st1[:], scalar1=1.0 / D, scalar2=None, op0=ALU.mult), reads=[st1], writes=[mean])
        m2 = tmp()
        kb.op("pool", lambda e: e.tensor_tensor(out=m2[:], in0=mean[:], in1=mean[:], op=ALU.mult), reads=[mean], writes=[m2])
        kb.op("dve", lambda e: e.scalar_tensor_tensor(out=rstd[:], in0=st2[:], scalar=1.0 / D, in1=m2[:], op0=ALU.mult, op1=ALU.subtract),
              reads=[st2, m2], writes=[rstd])
        kb.op("dve", lambda e: e.tensor_scalar(out=rstd[:], in0=rstd[:], scalar1=EPS, scalar2=None, op0=ALU.add), reads=[rstd], writes=[rstd])
        kb.op("act", lambda e: e.activation(out=rstd[:], in_=rstd[:], func=AF.Sqrt), reads=[rstd], writes=[rstd])
        kb.op("dve", lambda e: e.reciprocal(out=rstd[:], in_=rstd[:]), reads=[rstd], writes=[rstd])
        for c in range(8):
            tt_ = tmp()
            kb.op("pool", lambda e: e.tensor_tensor(out=tt_[:], in0=cvs[:, c, :], in1=mean[:], op=ALU.subtract), reads=[cvs, mean], writes=[tt_])
            kb.op("pool", lambda e: e.tensor_tensor(out=tt_[:], in0=tt_[:], in1=rstd[:], op=ALU.mult), reads=[tt_, rstd], writes=[tt_])
            kb.op("act", lambda e: e.activation(out=ycs[:, c, :], in_=tt_[:], func=AF.Silu, bias=pvs[:, LNB, c:c + 1], scale=pvs[:, LNG, c:c + 1]),
                  reads=[tt_, pvs], writes=[ycs])
        for nblk in range(4):
            cs = slice(nblk * 256, (nblk + 1) * 256)
            Wa = wload(woa[:, :, cs], 4, 256)
            Wl = wload(wol[:, :, cs], 8, 256)
            Wc = wload(woc[:, :, cs], 8, 256)
            Wg = [wload(wg[:, :, br * 1024 + nblk * 256: br * 1024 + (nblk + 1) * 256], 8, 256) for br in range(3)]
            for half in range(2):
                n = nblk * 2 + half
                col = half * 128
                for br, (Wb, src, kc) in enumerate(((Wa, os_, 4), (Wl, ybs, 8), (Wc, ycs, 8))):
                    ps_y = bank()
                    mmgroup(ps_y, Wb[0], Wb[1], col, src, kc)
                    ps_g = bank()
                    mmgroup(ps_g, Wg[br][0], Wg[br][1], col, hs, 8)
                    sg = tmp()
                    kb.op("act", lambda e: e.activation(out=sg[:], in_=ps_g[:], func=AF.Sigmoid), reads=[ps_g], writes=[sg])
                    if br == 0:
                        kb.op("dve", lambda e: e.tensor_tensor(out=macc[:], in0=ps_y[:], in1=sg[:], op=ALU.mult), reads=[ps_y, sg], writes=[macc])
                    else:
                        kb.op("dve", lambda e: e.tensor_tensor(out=sg[:], in0=ps_y[:], in1=sg[:], op=ALU.mult), reads=[ps_y, sg], writes=[sg])
                        if br == 1:
                            kb.op("pool", lambda e: e.tensor_tensor(out=macc[:], in0=macc[:], in1=sg[:], op=ALU.add), reads=[macc, sg], writes=[macc])
                        else:
                            kb.op("pool", lambda e: e.tensor_tensor(out=mg[:, n, :], in0=macc[:], in1=sg[:], op=ALU.add), reads=[macc, sg], writes=[mg])
            wsC.release(6)
        for nb2 in range(2):
            Wo = wload(wout[:, :, nb2 * 512:(nb2 + 1) * 512], 8, 512)
            for q4 in range(4):
                n = nb2 * 4 + q4
                ps = bank()
                mmgroup(ps, Wo[0], Wo[1], q4 * 128, mg, 8)
                kb.op("act", lambda e: e.activation(out=cvs[:, n, :], in_=ps[:], func=AF.Copy), reads=[ps], writes=[cvs])
                stat_acc(st1, ps[:], ps, n, square=True)
            wsC.release(1)
        rstd_from(st1, rstd)
        for n in range(8):
            tt_ = tmp()
            kb.op("dve", lambda e: e.scalar_tensor_tensor(out=tt_[:], in0=cvs[:, n, :], scalar=pvs[:, GPOST, n:n + 1], in1=rstd[:], op0=ALU.mult, op1=ALU.mult),
                  reads=[cvs, pvs, rstd], writes=[tt_])
            kb.op("pool", lambda e: e.tensor_tensor(out=xs[:, n, :], in0=xs[:, n, :], in1=tt_[:], op=ALU.add), reads=[xs, tt_], writes=[xs])
        for n in range(8):
            stat_acc(st2, xs[:, n, :], xs, n, square=True)
        rstd_from(st2, rstd)
        for n in range(8):
            kb.op("dve", lambda e: e.scalar_tensor_tensor(out=ycs[:, n, :], in0=xs[:, n, :], scalar=pvs[:, GMPRE, n:n + 1], in1=rstd[:], op0=ALU.mult, op1=ALU.mult),
                  reads=[xs, pvs, rstd], writes=[ycs])
        for ub in range(8):
            Wu = wload(wup[:, :, ub * 512:(ub + 1) * 512], 8, 512)
            for q4 in range(4):
                j = ub * 4 + q4
                ps = bank()
                mmgroup(ps, Wu[0], Wu[1], q4 * 128, ycs, 8)
                rl_ = tmp()
                kb.op("act", lambda e: e.activation(out=rl_[:], in_=ps[:], func=AF.Relu), reads=[ps], writes=[rl_])
                kb.op("pool", lambda e: e.tensor_tensor(out=u2[:, j, :], in0=rl_[:], in1=rl_[:], op=ALU.mult), reads=[rl_], writes=[u2])
            wsC.release(1)
        for n in range(8):
            Wd = wload(wdn[:, :, n * 128:(n + 1) * 128], 32, 128)
            ps = bank()
            mmgroup(ps, Wd[0], Wd[1], 0, u2, 32)
            kb.op("act", lambda e: e.activation(out=cvs[:, n, :], in_=ps[:], func=AF.Copy), reads=[ps], writes=[cvs])
            stat_acc(st1, ps[:], ps, n, square=True)
            wsC.release(1)
        rstd_from(st1, rstd)
        for n in range(8):
            tt_ = tmp()
            kb.op("dve", lambda e: e.scalar_tensor_tensor(out=tt_[:], in0=cvs[:, n, :], scalar=pvs[:, GMPOST, n:n + 1], in1=rstd[:], op0=ALU.mult, op1=ALU.mult),
                  reads=[cvs, pvs, rstd], writes=[tt_])
            kb.op("pool", lambda e: e.tensor_tensor(out=xs[:, n, :], in0=xs[:, n, :], in1=tt_[:], op=ALU.add), reads=[xs, tt_], writes=[xs])
        kb.dma("sp", xo[:, :, ts], xs[:], reads=[xs])
    return kb.finish()


def fm(a2d):
    tok, F = a2d.shape
    return np.ascontiguousarray(a2d.reshape(tok, F // 128, 128).transpose(2, 1, 0))


def unfm(a3d):
    p, C, tok = a3d.shape
    return np.ascontiguousarray(a3d.transpose(2, 1, 0).reshape(tok, C * p))


def wfm(w):
    K, N = w.shape
    return np.ascontiguousarray(w.reshape(K // 128, 128, N).transpose(1, 0, 2))


def colvec(v):
    return np.ascontiguousarray(v.reshape(-1, 128).T)


def la_weight(w_in_l):
    q = w_in_l[:, 0:1536]
    k = w_in_l[:, 1536:3072]
    v = w_in_l[:, 3072:4608]
    lx = w_in_l[:, 4608:5632]
    lg = w_in_l[:, 5632:6656]
    gv = w_in_l[:, 6656:7680]
    gg = w_in_l[:, 7680:8704]
    swap = np.arange(1536).reshape(24, 2, 32)[:, ::-1, :].reshape(-1)
    cols = []
    for m in (q, k):
        msw = m[:, swap]
        for j in range(12):
            cols.append(m[:, j * 128:(j + 1) * 128])
            cols.append(msw[:, j * 128:(j + 1) * 128])
    cols.append(v)
    cols.append(lx)
    cols.append(lg)
    for j in range(8):
        cols.append(gv[:, j * 128:(j + 1) * 128])
        cols.append(gg[:, j * 128:(j + 1) * 128])
    W = np.concatenate(cols, axis=1)
    assert W.shape[1] == NA
    return wfm(W)


def rope_consts():
    inv_freq = (10000.0 ** (-np.arange(0, 64, 2, dtype=np.float32) / 64)).astype(np.float32)
    c = np.zeros((128, 2), np.float32)
    for p in range(128):
        c[p, 0] = inv_freq[p % 32]
        c[p, 1] = -1.0 if (p // 32) % 2 == 0 else 1.0
    return c


_CACHE = {}


def get_prog(name):
    if name not in _CACHE:
        _CACHE[name] = {"LA": build_LA, "LB": build_LB, "LC": build_LC}[name]()
    return _CACHE[name]


def run_LA(x_all, positions, inputs, l):
    wA = la_weight(inputs["w_in"][l])
    gpre = colvec(inputs["norm_mix_pre"][l])
    cr = rope_consts()
    in_maps = []
    for c in range(8):
        b, j = divmod(c, 4)
        sl = slice(j * TOK, (j + 1) * TOK)
        in_maps.append({
            "xT": fm(x_all[b, sl]),
            "pos": np.ascontiguousarray(np.broadcast_to(positions[b, sl][None, :], (128, TOK))).astype(np.int32),
            "crope": cr, "gpre": gpre, "wA": wA,
        })
    res = run_bass_kernel_spmd(get_prog("LA"), in_maps, core_ids=list(range(8)))
    return res.results


def class_perm(a, d):
    Sx, F = a.shape
    return a.reshape(Sx // d, d, F).transpose(1, 0, 2).reshape(Sx, F)


def band_mask():
    k = np.arange(128)[:, None]
    q = np.arange(128)[None, :]
    m = np.concatenate([(k <= q), (k >= q)], axis=1)
    return m.astype(np.float32).astype(NPBF)


def blockdiag(w2):
    o = np.zeros((128, 128), np.float32)
    o[:64, :64] = w2[0]
    o[64:, 64:] = w2[1]
    return o


def run_LB(q_full, k_full, v_full, lx_full, gel_full, ug_full, inputs, l):
    msk = band_mask()
    in_maps = []
    for c in range(8):
        qa = np.zeros((64, 6, S), NPBF)
        ka = np.zeros((64, 6, S), NPBF)
        va = np.zeros((128, 6, 64, 64), NPBF)
        for b in range(B):
            for g in range(3):
                cs = slice(g * 512 + c * 64, g * 512 + (c + 1) * 64)
                bg = b * 3 + g
                qa[:, bg, :] = class_perm(q_full[b][:, cs], DIL[g]).T
                ka[:, bg, :] = class_perm(k_full[b][:, cs], DIL[g]).T
                va[:, bg, :, :] = class_perm(v_full[b][:, cs], DIL[g]).reshape(64, 128, 64).transpose(1, 0, 2)
        ch = slice(c * 128, (c + 1) * 128)
        lp = np.zeros((128, 8), np.float32)
        lp[:, 0:4] = inputs["lru_conv_w"][l][:, ch].T
        lp[:, 4] = inputs["lru_conv_b"][l][ch]
        lp[:, 5] = inputs["lru_ba"][l][ch]
        lp[:, 6] = inputs["lru_bx"][l][ch]
        lp[:, 7] = inputs["lru_lambda"][l][ch]
        wab = np.concatenate([blockdiag(inputs["lru_wa"][l][2 * c:2 * c + 2]), blockdiag(inputs["lru_wx"][l][2 * c:2 * c + 2])], axis=1)
        cw = np.zeros((128, 32), np.float32)
        cw[:, 0:31] = inputs["conv_dw_w"][l][:, ch].T
        cw[:, 31] = inputs["conv_dw_b"][l][ch]
        in_maps.append({
            "qa": qa, "ka": ka, "va": va, "cmask": msk,
            "lx": np.ascontiguousarray(lx_full[:, :, ch].transpose(2, 0, 1)).astype(np.float32),
            "gel": np.ascontiguousarray(gel_full[:, :, ch].transpose(2, 0, 1)).astype(np.float32),
            "ug": np.ascontiguousarray(ug_full[:, :, ch].transpose(2, 0, 1)).astype(np.float32),
            "lp": lp, "wab": np.ascontiguousarray(wab), "cw": cw,
        })
    res = run_bass_kernel_spmd(get_prog("LB"), in_maps, core_ids=list(range(8))).results
    o_full = np.zeros((B, S, 512), NPBF)
    yb_full = np.zeros((B, S, 1024), NPBF)
    cv_full = np.zeros((B, S, 1024), np.float32)
    for c in range(8):
        o_full[:, :, c * 64:(c + 1) * 64] = np.asarray(res[c]["oT"]).transpose(1, 2, 0)
        yb_full[:, :, c * 128:(c + 1) * 128] = np.asarray(res[c]["yb"]).transpose(1, 2, 0)
        cv_full[:, :, c * 128:(c + 1) * 128] = np.asarray(res[c]["cv"]).transpose(1, 2, 0)
    return o_full, yb_full, cv_full


def run_LC(x_all, h_full, o_full, yb_full, cv_full, inputs, l):
    pv = np.stack([colvec(inputs["conv_ln_g"][l]), colvec(inputs["conv_ln_b"][l]), colvec(inputs["norm_mix_post"][l]),
                   colvec(inputs["norm_mlp_pre"][l]), colvec(inputs["norm_mlp_post"][l])], axis=1).astype(np.float32)
    pv = np.ascontiguousarray(pv)
    wgm = wfm(np.ascontiguousarray(inputs["w_in"][l][:, 8704:11776]))
    woa = wfm(inputs["w_o_attn"][l])
    wol = wfm(inputs["w_o_lru"][l])
    woc = wfm(inputs["w_o_conv"][l])
    wout = wfm(inputs["w_out"][l])
    wup = wfm(inputs["w_mlp_up"][l])
    wdn = wfm(inputs["w_mlp_down"][l])
    in_maps = []
    for c in range(8):
        b, j = divmod(c, 4)
        sl = slice(j * TOK, (j + 1) * TOK)
        in_maps.append({
            "xT": fm(x_all[b, sl]), "hT": fm(h_full[b, sl]), "oT": fm(o_full[b, sl]), "ybT": fm(yb_full[b, sl]),
            "cvT": fm(cv_full[b, sl]), "pv": pv, "wg": wgm, "woa": woa, "wol": wol, "woc": woc, "wout": wout,
            "wup": wup, "wdn": wdn,
        })
    res = run_bass_kernel_spmd(get_prog("LC"), in_maps, core_ids=list(range(8))).results
    x_new = np.zeros((B, S, D), np.float32)
    for c in range(8):
        b, j = divmod(c, 4)
        x_new[b, j * TOK:(j + 1) * TOK] = unfm(np.asarray(res[c]["xo"]))
    return x_new


def gather_tok(res, name, dtype):
    C = np.asarray(res[0][name]).shape[1]
    out = np.zeros((B, S, C * 128), dtype)
    for c in range(8):
        b, j = divmod(c, 4)
        out[b, j * TOK:(j + 1) * TOK] = unfm(np.asarray(res[c][name]))
    return out


def kernel(**inputs):
    inputs = {k: np.asarray(v) for k, v in inputs.items()}
    x = np.ascontiguousarray(inputs["x"]).astype(np.float32)
    positions = inputs["positions"]
    for l in range(NL):
        res = run_LA(x, positions, inputs, l)
        q_full = gather_tok(res, "qT", NPBF)
        k_full = gather_tok(res, "kT", NPBF)
        v_full = gather_tok(res, "vT", NPBF)
        lx_full = gather_tok(res, "lxT", np.float32)
        gel_full = gather_tok(res, "gelT", np.float32)
        ug_full = gather_tok(res, "ugT", np.float32)
        h_full = gather_tok(res, "hT", NPBF)
        del res
        o_full, yb_full, cv_full = run_LB(q_full, k_full, v_full, lx_full, gel_full, ug_full, inputs, l)
        x = run_LC(x, h_full, o_full, yb_full, cv_full, inputs, l)
    return x.astype(np.float32)
```

```python
import numpy as np
import ml_dtypes
from contextlib import ExitStack
import concourse.bass as bass
import concourse.mybir as mybir
from concourse.bass_utils import run_bass_kernel_spmd

F32 = mybir.dt.float32
BF16 = mybir.dt.bfloat16
I32 = mybir.dt.int32
AF = mybir.ActivationFunctionType
ALU = mybir.AluOpType
NPBF = ml_dtypes.bfloat16

D = 1024
S = 8192
B = 2
NL = 4
TOK = 2048
T = 512
NT = TOK // T
EPS = 1e-6
PI = float(np.pi)


class Buf:
    __slots__ = ("t", "lastw", "readers", "name")

    def __init__(self, t, name=""):
        self.t = t
        self.lastw = None
        self.readers = {}
        self.name = name

    def __getitem__(self, k):
        return self.t[k]


class Eng:
    def __init__(self, name, e, sem, same_sync):
        self.name = name
        self.e = e
        self.sem = sem
        self.cnt = 0
        self.waited = {}
        self.same_sync = same_sync


class KB:
    def __init__(self, n_dma_sems=32):
        self.nc = bass.Bass("TRN2", target_bir_lowering=False)
        self.es = ExitStack()
        nc = self.nc
        self.engs = {}
        for name, e, ss in (("pe", nc.tensor, False), ("act", nc.scalar, True), ("dve", nc.vector, True),
                            ("pool", nc.gpsimd, True), ("sp", nc.sync, True)):
            sem = self.es.enter_context(nc.semaphore("s_" + name))
            self.engs[name] = Eng(name, e, sem, ss)
        self.dsems = []
        for i in range(n_dma_sems):
            sem = self.es.enter_context(nc.semaphore(f"s_dma{i}"))
            self.dsems.append([sem, 0])
        self.drr = 0
        self.semkeys = {}
        self.nbuf = 0
        self.psum_rr = 0
        self.psums = []
        self.out_tokens = []

    def sb(self, shape, dt, name=None, es=None):
        self.nbuf += 1
        name = name or f"b{self.nbuf}"
        t = (es or self.es).enter_context(self.nc.sbuf_tensor(f"{name}_{self.nbuf}", list(shape), dt))
        return Buf(t, name)

    def ps(self, shape, dt, name=None, es=None):
        self.nbuf += 1
        name = name or f"p{self.nbuf}"
        t = (es or self.es).enter_context(self.nc.psum_tensor(f"{name}_{self.nbuf}", list(shape), dt))
        return Buf(t, name)

    def dram(self, name, shape, dt, kind):
        t = self.nc.dram_tensor(name, list(shape), dt, kind=kind)
        b = Buf(t.ap(), name)
        return b

    def alloc_psum_banks(self):
        self.psums = [self.ps([128, 512], F32, f"bank{i}") for i in range(8)]

    def bank(self):
        b = self.psums[self.psum_rr]
        self.psum_rr = (self.psum_rr + 1) % 8
        return b

    def _wait(self, eng, tok):
        key, sem, val = tok
        if eng.waited.get(key, 0) >= val:
            return
        eng.e.wait_ge(sem, val)
        eng.waited[key] = val

    def _collect(self, reads, writes):
        need = {}

        def add(tok):
            if tok is None:
                return
            k = tok[0]
            if k not in need or need[k][2] < tok[2]:
                need[k] = tok

        for b in reads:
            add(b.lastw)
        for b in writes:
            add(b.lastw)
            for tok in b.readers.values():
                add(tok)
        return need

    def _mark(self, tok, reads, writes):
        for b in reads:
            b.readers[tok[0]] = tok
        for b in writes:
            b.lastw = tok
            b.readers = {}

    def op(self, ename, fn, reads=(), writes=(), inc=True):
        eng = self.engs[ename]
        need = self._collect(reads, writes)
        for k, tok in need.items():
            if k == eng.name and not eng.same_sync:
                continue
            self._wait(eng, tok)
        ins = fn(eng.e)
        if inc:
            ins.then_inc(eng.sem, 1)
            eng.cnt += 1
            tok = (eng.name, eng.sem, eng.cnt)
        else:
            tok = (eng.name, eng.sem, eng.cnt + 1)
        self._mark(tok, reads, writes)
        return tok

    def dma(self, q, out_ap, in_ap, reads=(), writes=()):
        eng = self.engs[q]
        need = self._collect(reads, writes)
        idx = self.drr
        self.drr = (self.drr + 1) % len(self.dsems)
        ds = self.dsems[idx]
        key = f"dma{idx}"
        if ds[1] > 0:
            need_prev = (key, ds[0], ds[1])
            if key not in need or need[key][2] < ds[1]:
                need[key] = need_prev
        for k, tok in need.items():
            self._wait(eng, tok)
        ds[1] += 16
        eng.e.dma_start(out=out_ap, in_=in_ap).then_inc(ds[0], 16)
        tok = (key, ds[0], ds[1])
        self._mark(tok, reads, writes)
        return tok

    def barrier(self):
        toks = []
        for e in self.engs.values():
            if e.cnt > 0:
                toks.append((e.name, e.sem, e.cnt))
        for i, ds in enumerate(self.dsems):
            if ds[1] > 0:
                toks.append((f"dma{i}", ds[0], ds[1]))
        for e in self.engs.values():
            for tok in toks:
                if tok[0] == e.name:
                    continue
                self._wait(e, tok)

    def finish(self):
        eng = self.engs["sp"]
        for i, ds in enumerate(self.dsems):
            if ds[1] > 0:
                self._wait(eng, (f"dma{i}", ds[0], ds[1]))
        self.es.close()
        return self.nc


class WStream:
    def __init__(self, kb, specs, nbuf, ahead, es=None, queue="pool"):
        self.kb = kb
        self.specs = specs
        self.nbuf = nbuf
        self.ahead = ahead
        self.queue = queue
        self.bufs = [kb.sb([128, 4096], BF16, f"ws{i}", es) for i in range(nbuf)]
        self.issued = 0
        self.released = 0
        self.cur = 0
        self.views = {}

    def _issue_upto(self, k):
        k = min(k, len(self.specs) - 1)
        while self.issued <= k:
            i = self.issued
            if i - self.nbuf >= self.released:
                break
            src, kc, ncols = self.specs[i]
            w_ = self.bufs[i % self.nbuf]
            view = w_[:, 0:kc * ncols].rearrange("p (k n) -> p k n", k=kc)
            self.kb.dma(self.queue, view, src, writes=[w_])
            self.views[i] = (w_, view)
            self.issued += 1

    def next(self):
        i = self.cur
        self.cur += 1
        self._issue_upto(i + self.ahead)
        assert i in self.views, (i, self.issued, self.released)
        w_, view = self.views.pop(i)
        return w_, view

    def release(self, n=1):
        self.released += n
        self._issue_upto(self.cur - 1 + self.ahead)


NA = 11776


def build_LA():
    kb = KB()
    nc = kb.nc
    xT = kb.dram("xT", [128, 8, TOK], F32, "ExternalInput")
    pos = kb.dram("pos", [128, TOK], I32, "ExternalInput")
    crope = kb.dram("crope", [128, 2], F32, "ExternalInput")
    gpre = kb.dram("gpre", [128, 8], F32, "ExternalInput")
    wA = kb.dram("wA", [128, 8, NA], F32, "ExternalInput")
    qT = kb.dram("qT", [128, 12, TOK], BF16, "ExternalOutput")
    kT = kb.dram("kT", [128, 12, TOK], BF16, "ExternalOutput")
    vT = kb.dram("vT", [128, 12, TOK], BF16, "ExternalOutput")
    lxT = kb.dram("lxT", [128, 8, TOK], F32, "ExternalOutput")
    gelT = kb.dram("gelT", [128, 8, TOK], F32, "ExternalOutput")
    ugT = kb.dram("ugT", [128, 8, TOK], F32, "ExternalOutput")
    hTo = kb.dram("hT", [128, 8, TOK], BF16, "ExternalOutput")

    kb.alloc_psum_banks()
    ones = kb.sb([128, 128], F32, "ones")
    kb.op("pool", lambda e: e.memset(ones[:], 1.0), writes=[ones])
    cr = kb.sb([128, 2], F32, "crope")
    kb.dma("sp", cr[:], crope[:], writes=[cr])
    gp = kb.sb([128, 8], F32, "gpre")
    kb.dma("sp", gp[:], gpre[:], writes=[gp])

    cosF = kb.sb([128, TOK], F32, "cosF")
    sinS = kb.sb([128, TOK], F32, "sinS")
    es2 = ExitStack()
    posi = kb.sb([128, TOK], I32, "posi", es2)
    ang = kb.sb([128, TOK], F32, "ang", es2)
    tmpa = kb.sb([128, TOK], F32, "tmpa", es2)
    kb.dma("sp", posi[:], pos[:], writes=[posi])
    kb.op("dve", lambda e: e.tensor_copy(out=ang[:], in_=posi[:]), reads=[posi], writes=[ang])
    kb.op("dve", lambda e: e.tensor_scalar(out=ang[:], in0=ang[:], scalar1=cr[:, 0:1], scalar2=None, op0=ALU.mult),
          reads=[ang, cr], writes=[ang])
    tq = kb.sb([128, TOK], F32, "tq", es2)
    tii = kb.sb([128, TOK], I32, "tii", es2)
    gg_ = kb.sb([128, TOK], F32, "gg", es2)

    def sin_turns(dst, shift):
        kb.op("dve", lambda e: e.tensor_scalar(out=tq[:], in0=ang[:], scalar1=1.0 / (2 * PI), scalar2=shift, op0=ALU.mult, op1=ALU.add),
              reads=[ang], writes=[tq])
        kb.op("dve", lambda e: e.tensor_copy(out=tii[:], in_=tq[:]), reads=[tq], writes=[tii])
        kb.op("dve", lambda e: e.tensor_copy(out=tmpa[:], in_=tii[:]), reads=[tii], writes=[tmpa])
        kb.op("dve", lambda e: e.tensor_tensor(out=tq[:], in0=tq[:], in1=tmpa[:], op=ALU.subtract), reads=[tq, tmpa], writes=[tq])
        kb.op("dve", lambda e: e.tensor_scalar(out=gg_[:], in0=tq[:], scalar1=0.5, scalar2=None, op0=ALU.is_gt), reads=[tq], writes=[gg_])
        kb.op("dve", lambda e: e.tensor_tensor(out=tq[:], in0=tq[:], in1=gg_[:], op=ALU.subtract), reads=[tq, gg_], writes=[tq])
        kb.op("dve", lambda e: e.tensor_scalar(out=gg_[:], in0=tq[:], scalar1=-0.5, scalar2=None, op0=ALU.is_lt), reads=[tq], writes=[gg_])
        kb.op("dve", lambda e: e.tensor_tensor(out=tq[:], in0=tq[:], in1=gg_[:], op=ALU.add), reads=[tq, gg_], writes=[tq])
        kb.op("act", lambda e: e.activation(out=dst[:], in_=tq[:], func=AF.Sin, scale=2 * PI), reads=[tq], writes=[dst])

    sin_turns(tmpa, 0.0)
    kb.op("dve", lambda e: e.tensor_scalar(out=sinS[:], in0=tmpa[:], scalar1=cr[:, 1:2], scalar2=None, op0=ALU.mult),
          reads=[tmpa, cr], writes=[sinS])
    sin_turns(cosF, 0.25)
    kb.barrier()
    es2.close()

    hT = [kb.sb([128, 8, T], BF16, f"hT{t}") for t in range(NT)]
    es3 = ExitStack()
    xpool = [kb.sb([128, 8, T], F32, f"x{i}", es3) for i in range(2)]
    sqpool = [kb.sb([128, T], F32, f"sq{i}", es3) for i in range(3)]
    sd = kb.sb([128, T], F32, "sd", es3)
    rstd = kb.sb([128, T], F32, "rstd", es3)
    for t in range(NT):
        xb = xpool[t % 2]
        kb.dma("sp", xb[:], xT[:, :, t * T:(t + 1) * T], writes=[xb])
        ps = kb.bank()
        for c in range(8):
            sq = sqpool[c % 3]
            kb.op("act", lambda e, c=c, sq=sq: e.activation(out=sq[:], in_=xb[:, c, :], func=AF.Square),
                  reads=[xb], writes=[sq])
            kb.op("pe", lambda e, c=c, sq=sq: e.matmul(ps[:], ones[:], sq[:], start=(c == 0), stop=(c == 7)),
                  reads=[ones, sq], writes=[ps])
        kb.op("dve", lambda e: e.tensor_scalar(out=sd[:], in0=ps[:], scalar1=1.0 / D, scalar2=EPS, op0=ALU.mult, op1=ALU.add),
              reads=[ps], writes=[sd])
        kb.op("act", lambda e: e.activation(out=sd[:], in_=sd[:], func=AF.Sqrt), reads=[sd], writes=[sd])
        kb.op("dve", lambda e: e.reciprocal(out=rstd[:], in_=sd[:]), reads=[sd], writes=[rstd])
        for c in range(8):
            kb.op("dve", lambda e, c=c: e.scalar_tensor_tensor(out=hT[t][:, c, :], in0=xb[:, c, :], scalar=gp[:, c:c + 1],
                                                            in1=rstd[:], op0=ALU.mult, op1=ALU.mult),
                  reads=[xb, gp, rstd], writes=[hT[t]])
        kb.dma("sp", hTo[:, :, t * T:(t + 1) * T], hT[t][:], reads=[hT[t]])
    kb.barrier()
    es3.close()

    nblk = NA // 512
    wsA = WStream(kb, [(wA[:, :, blk * 512:(blk + 1) * 512], 8, 512) for blk in range(nblk)], 4, 2)
    obf = [kb.sb([128, T], BF16, f"obf{i}") for i in range(4)]
    of32 = [kb.sb([128, T], F32, f"of{i}") for i in range(4)]
    tmp1 = [kb.sb([128, T], F32, f"t1_{i}") for i in range(3)]
    tmp2 = [kb.sb([128, T], F32, f"t2_{i}") for i in range(3)]
    cnt = {"obf": 0, "of": 0, "tmp": 0}

    def mm(wb_unused, col0, t):
        ps = kb.bank()
        for kc in range(8):
            kb.op("pe", lambda e, kc=kc: e.matmul(ps[:], wb[:, kc, col0:col0 + 128], hT[t][:, kc, :], start=(kc == 0), stop=(kc == 7)),
                  reads=[wbuf_, hT[t]], writes=[ps], inc=(kc == 7))
        return ps

    for blk in range(nblk):
        if blk > 0:
            wsA.release(1)
        wbuf_, wb = wsA.next()
        if blk < 12:
            for u in range(2):
                unit = blk * 2 + u
                dst = qT if unit < 12 else kT
                j = unit % 12
                for t in range(NT):
                    psA = mm(wb, u * 256, t)
                    psB = mm(wb, u * 256 + 128, t)
                    i = cnt["tmp"]; cnt["tmp"] += 1
                    m1 = tmp1[i % 3]; m2 = tmp2[i % 3]
                    kb.op("dve", lambda e: e.tensor_tensor(out=m1[:], in0=psA[:], in1=cosF[:, t * T:(t + 1) * T], op=ALU.mult),
                          reads=[psA, cosF], writes=[m1])
                    kb.op("dve", lambda e: e.tensor_tensor(out=m2[:], in0=psB[:], in1=sinS[:, t * T:(t + 1) * T], op=ALU.mult),
                          reads=[psB, sinS], writes=[m2])
                    ob = obf[cnt["obf"] % 4]; cnt["obf"] += 1
                    kb.op("pool", lambda e: e.tensor_tensor(out=ob[:], in0=m1[:], in1=m2[:], op=ALU.add),
                          reads=[m1, m2], writes=[ob])
                    kb.dma("sp", dst[:, j, t * T:(t + 1) * T], ob[:], reads=[ob])
        elif blk < 15:
            for u in range(4):
                j = (blk - 12) * 4 + u
                for t in range(NT):
                    ps = mm(wb, u * 128, t)
                    ob = obf[cnt["obf"] % 4]; cnt["obf"] += 1
                    kb.op("act", lambda e: e.activation(out=ob[:], in_=ps[:], func=AF.Copy), reads=[ps], writes=[ob])
                    kb.dma("sp", vT[:, j, t * T:(t + 1) * T], ob[:], reads=[ob])
        elif blk < 19:
            for u in range(4):
                j = ((blk - 15) % 2) * 4 + u
                isx = blk < 17
                dst = lxT if isx else gelT
                for t in range(NT):
                    ps = mm(wb, u * 128, t)
                    ob = of32[cnt["of"] % 4]; cnt["of"] += 1
                    kb.op("act", lambda e: e.activation(out=ob[:], in_=ps[:], func=(AF.Copy if isx else AF.Gelu)),
                          reads=[ps], writes=[ob])
                    kb.dma("sp", dst[:, j, t * T:(t + 1) * T], ob[:], reads=[ob])
        else:
            for u in range(2):
                j = (blk - 19) * 2 + u
                for t in range(NT):
                    psA = mm(wb, u * 256, t)
                    psB = mm(wb, u * 256 + 128, t)
                    i = cnt["tmp"]; cnt["tmp"] += 1
                    sg = tmp1[i % 3]
                    kb.op("act", lambda e: e.activation(out=sg[:], in_=psB[:], func=AF.Sigmoid), reads=[psB], writes=[sg])
                    ob = of32[cnt["of"] % 4]; cnt["of"] += 1
                    kb.op("dve", lambda e: e.tensor_tensor(out=ob[:], in0=psA[:], in1=sg[:], op=ALU.mult),
                          reads=[psA, sg], writes=[ob])
                    kb.dma("sp", ugT[:, j, t * T:(t + 1) * T], ob[:], reads=[ob])
    return kb.finish()


DIL = (1, 4, 16)
TL = 2048


def build_LB():
    kb = KB()
    qa = kb.dram("qa", [64, 6, S], BF16, "ExternalInput")
    ka = kb.dram("ka", [64, 6, S], BF16, "ExternalInput")
    va = kb.dram("va", [128, 6, 64, 64], BF16, "ExternalInput")
    cmask = kb.dram("cmask", [128, 256], BF16, "ExternalInput")
    lx = kb.dram("lx", [128, 2, S], F32, "ExternalInput")
    gel = kb.dram("gel", [128, 2, S], F32, "ExternalInput")
    ug = kb.dram("ug", [128, 2, S], F32, "ExternalInput")
    lp = kb.dram("lp", [128, 8], F32, "ExternalInput")
    wab = kb.dram("wab", [128, 256], F32, "ExternalInput")
    cw = kb.dram("cw", [128, 32], F32, "ExternalInput")
    oT = kb.dram("oT", [64, 2, S], BF16, "ExternalOutput")
    yb = kb.dram("yb", [128, 2, S], BF16, "ExternalOutput")
    cv = kb.dram("cv", [128, 2, S], F32, "ExternalOutput")

    kb.alloc_psum_banks()
    onesb = kb.sb([128, 64], BF16, "onesb")
    kb.op("pool", lambda e: e.memset(onesb[:], 1.0), writes=[onesb])
    one1 = kb.sb([128, 1], F32, "one1")
    kb.op("pool", lambda e: e.memset(one1[:], 1.0), writes=[one1])
    zero1 = kb.sb([128, 1], F32, "zero1")
    kb.op("pool", lambda e: e.memset(zero1[:], 0.0), writes=[zero1])
    msk = kb.sb([128, 256], BF16, "msk")
    kb.dma("sp", msk[:], cmask[:], writes=[msk])
    lps = kb.sb([128, 8], F32, "lps")
    kb.dma("sp", lps[:], lp[:], writes=[lps])
    cws = kb.sb([128, 32], F32, "cws")
    kb.dma("sp", cws[:], cw[:], writes=[cws])
    wabs = kb.sb([128, 256], BF16, "wabs")
    kb.dma("pool", wabs[:], wab[:], writes=[wabs])

    esC = ExitStack()
    X2 = [kb.sb([128, 30 + TL], F32, f"X2{i}", esC) for i in range(2)]
    ACC = [kb.sb([128, TL], F32, f"ACC{i}", esC) for i in range(2)]
    nt = S // TL

    def conv_ops():
        for ti in range(2 * nt):
            b, tt = divmod(ti, nt)
            sl = slice(tt * TL, (tt + 1) * TL)
            Xc, Xp = X2[ti % 2], X2[(ti + 1) % 2]
            acc = ACC[ti % 2]
            if tt == 0:
                kb.op("pool", lambda e: e.memset(Xc[:, 0:30], 0.0), writes=[Xc])
            else:
                kb.op("pool", lambda e: e.tensor_copy(out=Xc[:, 0:30], in_=Xp[:, TL:TL + 30]), reads=[Xp], writes=[Xc])
            kb.dma("sp", Xc[:, 30:30 + TL], ug[:, b, sl], writes=[Xc])
            kb.op("dve", lambda e: e.tensor_scalar(out=acc[:], in0=Xc[:, 0:TL], scalar1=cws[:, 0:1], scalar2=cws[:, 31:32], op0=ALU.mult, op1=ALU.add),
                  reads=[Xc, cws], writes=[acc])
            yield
            for k in range(1, 31):
                kb.op("dve", lambda e, k=k: e.scalar_tensor_tensor(out=acc[:], in0=Xc[:, k:k + TL], scalar=cws[:, k:k + 1], in1=acc[:], op0=ALU.mult, op1=ALU.add),
                      reads=[Xc, cws, acc], writes=[acc])
                if k == 30:
                    kb.dma("sp", cv[:, b, sl], acc[:], reads=[acc])
                yield

    conv_it = conv_ops()

    esA = ExitStack()
    Oacc = kb.sb([64, S], F32, "Oacc", esA)
    Lacc = kb.sb([64, S], F32, "Lacc", esA)
    qs = [kb.sb([64, S], BF16, f"qs{i}", esA) for i in range(2)]
    ks = [kb.sb([64, S], BF16, f"ks{i}", esA) for i in range(2)]
    vs = [kb.sb([128, 64, 64], BF16, f"vs{i}", esA) for i in range(2)]
    pts = [kb.sb([128, 256], BF16, f"pt{i}", esA) for i in range(4)]
    ofin = [kb.sb([64, 2048], BF16, f"ofin{i}", esA) for i in range(2)]
    rl = kb.sb([64, 2048], F32, "rl", esA)
    for bg in range(6):
        b, g = divmod(bg, 3)
        d = DIL[g]
        nb = 64 // d
        q_sb, k_sb, v_sb = qs[bg % 2], ks[bg % 2], vs[bg % 2]
        kb.dma("sp", q_sb[:], qa[:, bg, :], writes=[q_sb])
        kb.dma("sp", k_sb[:], ka[:, bg, :], writes=[k_sb])
        kb.dma("sp", v_sb[:], va[:, bg, :, :], writes=[v_sb])
        Ov = Oacc[:, :].rearrange("p (l d) -> p d l", d=d)
        Lv = Lacc[:, :].rearrange("p (l d) -> p d l", d=d)
        prev_pt = None
        for i in range(64):
            r, j = divmod(i, nb)
            nq = 256 if j != nb - 1 else 128
            sT = kb.bank()
            kb.op("pe", lambda e: e.matmul(sT[:, :nq], k_sb[:, i * 128:(i + 1) * 128], q_sb[:, i * 128:i * 128 + nq], start=True, stop=True),
                  reads=[k_sb, q_sb], writes=[sT])
            pt = pts[i % 4]
            kb.op("act", lambda e: e.activation(out=pt[:, :nq], in_=sT[:, :nq], func=AF.Exp, scale=0.125), reads=[sT], writes=[pt])
            kb.op("pool", lambda e: e.tensor_tensor(out=pt[:, :nq], in0=pt[:, :nq], in1=msk[:, :nq], op=ALU.mult), reads=[pt, msk], writes=[pt])
            o_ps = kb.bank()
            l_ps = kb.bank()
            if j > 0:
                kb.op("pe", lambda e: e.matmul(o_ps[:64, :128], v_sb[:, i - 1, :], prev_pt[:, 128:256], start=True, stop=False),
                      reads=[v_sb, prev_pt], writes=[o_ps], inc=False)
                kb.op("pe", lambda e: e.matmul(l_ps[:64, :128], onesb[:, :], prev_pt[:, 128:256], start=True, stop=False),
                      reads=[onesb, prev_pt], writes=[l_ps], inc=False)
            kb.op("pe", lambda e: e.matmul(o_ps[:64, :128], v_sb[:, i, :], pt[:, 0:128], start=(j == 0), stop=True),
                  reads=[v_sb, pt], writes=[o_ps])
            kb.op("pe", lambda e: e.matmul(l_ps[:64, :128], onesb[:, :], pt[:, 0:128], start=(j == 0), stop=True),
                  reads=[onesb, pt], writes=[l_ps])
            od = Ov[:, r, j * 128:(j + 1) * 128]
            ld = Lv[:, r, j * 128:(j + 1) * 128]
            if g == 0:
                kb.op("act", lambda e: e.activation(out=od, in_=o_ps[:64, :128], func=AF.Copy), reads=[o_ps], writes=[Oacc])
                kb.op("dve", lambda e: e.tensor_copy(out=ld, in_=l_ps[:64, :128]), reads=[l_ps], writes=[Lacc])
            else:
                kb.op("dve", lambda e: e.tensor_tensor(out=od, in0=o_ps[:64, :128], in1=od, op=ALU.add), reads=[o_ps, Oacc], writes=[Oacc])
                kb.op("dve", lambda e: e.tensor_tensor(out=ld, in0=l_ps[:64, :128], in1=ld, op=ALU.add), reads=[l_ps, Lacc], writes=[Lacc])
            prev_pt = pt
            next(conv_it, None)
        if g == 2:
            for piece in range(4):
                sl = slice(piece * 2048, (piece + 1) * 2048)
                of_ = ofin[piece % 2]
                kb.op("dve", lambda e: e.reciprocal(out=rl[:], in_=Lacc[:, sl]), reads=[Lacc], writes=[rl])
                kb.op("dve", lambda e: e.tensor_tensor(out=of_[:], in0=Oacc[:, sl], in1=rl[:], op=ALU.mult), reads=[Oacc, rl], writes=[of_])
                kb.dma("sp", oT[:, b, sl], of_[:], reads=[of_])
    for _ in conv_it:
        pass
    kb.barrier()
    esA.close()

    esL = ExitStack()
    ca = kb.sb([128, 1], F32, "ca", esL)
    kb.op("act", lambda e: e.activation(out=ca[:], in_=lps[:, 7:8], func=AF.Exp, scale=-1.0), reads=[lps], writes=[ca])
    kb.op("act", lambda e: e.activation(out=ca[:], in_=ca[:], func=AF.Ln, bias=one1[:, 0:1], scale=1.0), reads=[ca, one1], writes=[ca])
    kb.op("dve", lambda e: e.tensor_scalar(out=ca[:], in0=ca[:], scalar1=-8.0, scalar2=None, op0=ALU.mult), reads=[ca], writes=[ca])
    X = [kb.sb([128, 3 + TL], F32, f"X{i}", esL) for i in range(2)]
    H = [kb.sb([128, TL], F32, f"H{i}", esL) for i in range(2)]
    U = kb.sb([128, TL], F32, "U", esL)
    UB = kb.sb([128, TL], BF16, "UB", esL)
    Rr = kb.sb([128, TL], F32, "Rr", esL)
    Ii = kb.sb([128, TL], F32, "Ii", esL)
    Aa = kb.sb([128, TL], F32, "Aa", esL)
    Ss = kb.sb([128, TL], F32, "Ss", esL)
    Gg = kb.sb([128, TL], F32, "Gg", esL)
    Yy = [kb.sb([128, TL], BF16, f"Yy{i}", esL) for i in range(2)]
    nt = S // TL
    for ti in range(2 * nt):
        b, tt = divmod(ti, nt)
        sl = slice(tt * TL, (tt + 1) * TL)
        Xc, Xp = X[ti % 2], X[(ti + 1) % 2]
        Hc, Hp = H[ti % 2], H[(ti + 1) % 2]
        if tt == 0:
            kb.op("pool", lambda e: e.memset(Xc[:, 0:3], 0.0), writes=[Xc])
        else:
            kb.op("pool", lambda e: e.tensor_copy(out=Xc[:, 0:3], in_=Xp[:, TL:TL + 3]), reads=[Xp], writes=[Xc])
        kb.dma("sp", Xc[:, 3:3 + TL], lx[:, b, sl], writes=[Xc])
        kb.dma("sp", Gg[:], gel[:, b, sl], writes=[Gg])
        kb.op("dve", lambda e: e.tensor_scalar(out=U[:], in0=Xc[:, 0:TL], scalar1=lps[:, 0:1], scalar2=lps[:, 4:5], op0=ALU.mult, op1=ALU.add),
              reads=[Xc, lps], writes=[U])
        for k in range(1, 4):
            kb.op("dve", lambda e, k=k: e.scalar_tensor_tensor(out=U[:], in0=Xc[:, k:k + TL], scalar=lps[:, k:k + 1], in1=U[:], op0=ALU.mult, op1=ALU.add),
                  reads=[Xc, lps, U], writes=[U])
        kb.op("pool", lambda e: e.tensor_copy(out=UB[:], in_=U[:]), reads=[U], writes=[UB])
        for s4 in range(TL // 512):
            ss = slice(s4 * 512, (s4 + 1) * 512)
            psr = kb.bank()
            kb.op("pe", lambda e: e.matmul(psr[:], wabs[:, 0:128], UB[:, ss], start=True, stop=True), reads=[wabs, UB], writes=[psr])
            psi = kb.bank()
            kb.op("pe", lambda e: e.matmul(psi[:], wabs[:, 128:256], UB[:, ss], start=True, stop=True), reads=[wabs, UB], writes=[psi])
            kb.op("act", lambda e: e.activation(out=Rr[:, ss], in_=psr[:], func=AF.Sigmoid, bias=lps[:, 5:6], scale=1.0), reads=[psr, lps], writes=[Rr])
            kb.op("act", lambda e: e.activation(out=Ii[:, ss], in_=psi[:], func=AF.Sigmoid, bias=lps[:, 6:7], scale=1.0), reads=[psi, lps], writes=[Ii])
        kb.op("act", lambda e: e.activation(out=Aa[:], in_=Rr[:], func=AF.Exp, scale=ca[:, 0:1]), reads=[Rr, ca], writes=[Aa])
        kb.op("pool", lambda e: e.tensor_tensor(out=Ss[:], in0=Aa[:], in1=Aa[:], op=ALU.mult), reads=[Aa], writes=[Ss])
        kb.op("act", lambda e: e.activation(out=Ss[:], in_=Ss[:], func=AF.Sqrt, bias=one1[:, 0:1], scale=-1.0), reads=[Ss, one1], writes=[Ss])
        kb.op("pool", lambda e: e.tensor_tensor(out=Ss[:], in0=Ss[:], in1=Ii[:], op=ALU.mult), reads=[Ss, Ii], writes=[Ss])
        kb.op("pool", lambda e: e.tensor_tensor(out=Ss[:], in0=Ss[:], in1=U[:], op=ALU.mult), reads=[Ss, U], writes=[Ss])
        init = zero1[:, 0:1] if tt == 0 else Hp[:, TL - 1:TL]
        kb.op("dve", lambda e: e.tensor_tensor_scan(out=Hc[:], data0=Aa[:], data1=Ss[:], initial=init, op0=ALU.mult, op1=ALU.add),
              reads=[Aa, Ss, Hp, zero1], writes=[Hc])
        Yc = Yy[ti % 2]
        kb.op("pool", lambda e: e.tensor_tensor(out=Yc[:], in0=Hc[:], in1=Gg[:], op=ALU.mult), reads=[Hc, Gg], writes=[Yc])
        kb.dma("sp", yb[:, b, sl], Yc[:], reads=[Yc])
    kb.barrier()
    esL.close()

    kb.barrier()
    esC.close()
    return kb.finish()


def build_LC():
    kb = KB()
    xT = kb.dram("xT", [128, 8, TOK], F32, "ExternalInput")
    hTi = kb.dram("hT", [128, 8, TOK], BF16, "ExternalInput")
    oTi = kb.dram("oT", [128, 4, TOK], BF16, "ExternalInput")
    ybi = kb.dram("ybT", [128, 8, TOK], BF16, "ExternalInput")
    cvi = kb.dram("cvT", [128, 8, TOK], F32, "ExternalInput")
    pv = kb.dram("pv", [128, 5, 8], F32, "ExternalInput")
    wg = kb.dram("wg", [128, 8, 3072], F32, "ExternalInput")
    woa = kb.dram("woa", [128, 4, 1024], F32, "ExternalInput")
    wol = kb.dram("wol", [128, 8, 1024], F32, "ExternalInput")
    woc = kb.dram("woc", [128, 8, 1024], F32, "ExternalInput")
    wout = kb.dram("wout", [128, 8, 1024], F32, "ExternalInput")
    wup = kb.dram("wup", [128, 8, 4096], F32, "ExternalInput")
    wdn = kb.dram("wdn", [128, 32, 1024], F32, "ExternalInput")
    xo = kb.dram("xo", [128, 8, TOK], F32, "ExternalOutput")

    banks = [kb.ps([128, 512], F32, f"bank{i}") for i in range(8)]
    st1, st2 = banks[6], banks[7]
    rr = {"b": 0, "w": 0, "t": 0}

    def bank():
        b_ = banks[rr["b"] % 6]
        rr["b"] += 1
        return b_

    ones = kb.sb([128, 128], F32, "ones")
    kb.op("pool", lambda e: e.memset(ones[:], 1.0), writes=[ones])
    pvs = kb.sb([128, 5, 8], F32, "pvs")
    kb.dma("sp", pvs[:], pv[:], writes=[pvs])
    LNG, LNB, GPOST, GMPRE, GMPOST = range(5)

    xs = kb.sb([128, 8, T], F32, "xs")
    hs = kb.sb([128, 8, T], BF16, "hs")
    os_ = kb.sb([128, 4, T], BF16, "os")
    ybs = kb.sb([128, 8, T], BF16, "ybs")
    cvs = kb.sb([128, 8, T], F32, "cvs")
    ycs = kb.sb([128, 8, T], BF16, "ycs")
    mg = kb.sb([128, 8, T], BF16, "mg")
    u2 = kb.sb([128, 32, T], BF16, "u2")
    specs = []
    for t_ in range(NT):
        for nblk in range(4):
            cs_ = slice(nblk * 256, (nblk + 1) * 256)
            specs.append((woa[:, :, cs_], 4, 256))
            specs.append((wol[:, :, cs_], 8, 256))
            specs.append((woc[:, :, cs_], 8, 256))
            for br in range(3):
                specs.append((wg[:, :, br * 1024 + nblk * 256: br * 1024 + (nblk + 1) * 256], 8, 256))
        for nb2 in range(2):
            specs.append((wout[:, :, nb2 * 512:(nb2 + 1) * 512], 8, 512))
        for ub in range(8):
            specs.append((wup[:, :, ub * 512:(ub + 1) * 512], 8, 512))
        for n_ in range(8):
            specs.append((wdn[:, :, n_ * 128:(n_ + 1) * 128], 32, 128))
    wsC = WStream(kb, specs, 10, 6)
    tp = [kb.sb([128, T], F32, f"tp{i}") for i in range(6)]
    mean = kb.sb([128, T], F32, "mean")
    rstd = kb.sb([128, T], F32, "rstd")
    macc = kb.sb([128, T], F32, "macc")

    def tmp():
        t_ = tp[rr["t"] % 6]
        rr["t"] += 1
        return t_

    def wload(src_ap, kc, ncols):
        i = wsC.cur
        assert wsC.specs[i][1] == kc and wsC.specs[i][2] == ncols
        return wsC.next()

    def mmgroup(ps, wbuf, wview, col0, rhs_buf, kc):
        for k in range(kc):
            kb.op("pe", lambda e, k=k: e.matmul(ps[:], wview[:, k, col0:col0 + 128], rhs_buf[:, k, :], start=(k == 0), stop=(k == kc - 1)),
                  reads=[wbuf, rhs_buf], writes=[ps], inc=(k == kc - 1))

    def rstd_from(stb, dst):
        kb.op("dve", lambda e: e.tensor_scalar(out=dst[:], in0=stb[:], scalar1=1.0 / D, scalar2=EPS, op0=ALU.mult, op1=ALU.add),
              reads=[stb], writes=[dst])
        kb.op("act", lambda e: e.activation(out=dst[:], in_=dst[:], func=AF.Sqrt), reads=[dst], writes=[dst])
        kb.op("dve", lambda e: e.reciprocal(out=dst[:], in_=dst[:]), reads=[dst], writes=[dst])

    def stat_acc(stb, src_ap, src_buf, n, square=True):
        if square:
            sq = tmp()
            kb.op("act", lambda e: e.activation(out=sq[:], in_=src_ap, func=AF.Square), reads=[src_buf], writes=[sq])
            kb.op("pe", lambda e: e.matmul(stb[:], ones[:], sq[:], start=(n == 0), stop=(n == 7)), reads=[ones, sq], writes=[stb])
        else:
            kb.op("pe", lambda e: e.matmul(stb[:], ones[:], src_ap, start=(n == 0), stop=(n == 7)), reads=[ones, src_buf], writes=[stb])

    for t in range(NT):
        ts = slice(t * T, (t + 1) * T)
        kb.dma("sp", xs[:], xT[:, :, ts], writes=[xs])
        kb.dma("sp", hs[:], hTi[:, :, ts], writes=[hs])
        kb.dma("sp", os_[:], oTi[:, :, ts], writes=[os_])
        kb.dma("sp", ybs[:], ybi[:, :, ts], writes=[ybs])
        kb.dma("sp", cvs[:], cvi[:, :, ts], writes=[cvs])
        for c in range(8):
            stat_acc(st1, cvs[:, c, :], cvs, c, square=False)
            stat_acc(st2, cvs[:, c, :], cvs, c, square=True)
        kb.op("dve", lambda e: e.tensor_scalar(out=mean[:], in0=st1[:], scalar1=1.0 / D, scalar2=None, op0=ALU.mult), reads=[st1], writes=[mean])
        m2 = tmp()
        kb.op("pool", lambda e: e.tensor_tensor(out=m2[:], in0=mean[:], in1=mean[:], op=ALU.mult), reads=[mean], writes=[m2])
        kb.op("dve", lambda e: e.scalar_tensor_tensor(out=rstd[:], in0=st2[:], scalar=1.0 / D, in1=m2[:], op0=ALU.mult, op1=ALU.subtract),
              reads=[st2, m2], writes=[rstd])
        kb.op("dve", lambda e: e.tensor_scalar(out=rstd[:], in0=rstd[:], scalar1=EPS, scalar2=None, op0=ALU.add), reads=[rstd], writes=[rstd])
        kb.op("act", lambda e: e.activation(out=rstd[:], in_=rstd[:], func=AF.Sqrt), reads=[rstd], writes=[rstd])
        kb.op("dve", lambda e: e.reciprocal(out=rstd[:], in_=rstd[:]), reads=[rstd], writes=[rstd])
        for c in range(8):
            tt_ = tmp()
            kb.op("pool", lambda e: e.tensor_tensor(out=tt_[:], in0=cvs[:, c, :], in1=mean[:], op=ALU.subtract), reads=[cvs, mean], writes=[tt_])
            kb.op("pool", lambda e: e.tensor_tensor(out=tt_[:], in0=tt_[:], in1=rstd[:], op=ALU.mult), reads=[tt_, rstd], writes=[tt_])
            kb.op("act", lambda e: e.activation(out=ycs[:, c, :], in_=tt_[:], func=AF.Silu, bias=pvs[:, LNB, c:c + 1], scale=pvs[:, LNG, c:c + 1]),
                  reads=[tt_, pvs], writes=[ycs])
        for nblk in range(4):
            cs = slice(nblk * 256, (nblk + 1) * 256)
            Wa = wload(woa[:, :, cs], 4, 256)
            Wl = wload(wol[:, :, cs], 8, 256)
            Wc = wload(woc[:, :, cs], 8, 256)
            Wg = [wload(wg[:, :, br * 1024 + nblk * 256: br * 1024 + (nblk + 1) * 256], 8, 256) for br in range(3)]
            for half in range(2):
                n = nblk * 2 + half
                col = half * 128
                for br, (Wb, src, kc) in enumerate(((Wa, os_, 4), (Wl, ybs, 8), (Wc, ycs, 8))):
                    ps_y = bank()
                    mmgroup(ps_y, Wb[0], Wb[1], col, src, kc)
                    ps_g = bank()
                    mmgroup(ps_g, Wg[br][0], Wg[br][1], col, hs, 8)
                    sg = tmp()
                    kb.op("act", lambda e: e.activation(out=sg[:], in_=ps_g[:], func=AF.Sigmoid), reads=[ps_g], writes=[sg])
                    if br == 0:
                        kb.op("dve", lambda e: e.tensor_tensor(out=macc[:], in0=ps_y[:], in1=sg[:], op=ALU.mult), reads=[ps_y, sg], writes=[macc])
                    else:
                        kb.op("dve", lambda e: e.tensor_tensor(out=sg[:], in0=ps_y[:], in1=sg[:], op=ALU.mult), reads=[ps_y, sg], writes=[sg])
                        if br == 1:
                            kb.op("pool", lambda e: e.tensor_tensor(out=macc[:], in0=macc[:], in1=sg[:], op=ALU.add), reads=[macc, sg], writes=[macc])
                        else:
                            kb.op("pool", lambda e: e.tensor_tensor(out=mg[:, n, :], in0=macc[:], in1=sg[:], op=ALU.add), reads=[macc, sg], writes=[mg])
            wsC.release(6)
        for nb2 in range(2):
            Wo = wload(wout[:, :, nb2 * 512:(nb2 + 1) * 512], 8, 512)
            for q4 in range(4):
                n = nb2 * 4 + q4
                ps = bank()
                mmgroup(ps, Wo[0], Wo[1], q4 * 128, mg, 8)
                kb.op("act", lambda e: e.activation(out=cvs[:, n, :], in_=ps[:], func=AF.Copy), reads=[ps], writes=[cvs])
                stat_acc(st1, ps[:], ps, n, square=True)
            wsC.release(1)
        rstd_from(st1, rstd)
        for n in range(8):
            tt_ = tmp()
            kb.op("dve", lambda e: e.scalar_tensor_tensor(out=tt_[:], in0=cvs[:, n, :], scalar=pvs[:, GPOST, n:n + 1], in1=rstd[:], op0=ALU.mult, op1=ALU.mult),
                  reads=[cvs, pvs, rstd], writes=[tt_])
            kb.op("pool", lambda e: e.tensor_tensor(out=xs[:, n, :], in0=xs[:, n, :], in1=tt_[:], op=ALU.add), reads=[xs, tt_], writes=[xs])
        for n in range(8):
            stat_acc(st2, xs[:, n, :], xs, n, square=True)
        rstd_from(st2, rstd)
        for n in range(8):
            kb.op("dve", lambda e: e.scalar_tensor_tensor(out=ycs[:, n, :], in0=xs[:, n, :], scalar=pvs[:, GMPRE, n:n + 1], in1=rstd[:], op0=ALU.mult, op1=ALU.mult),
                  reads=[xs, pvs, rstd], writes=[ycs])
        for ub in range(8):
            Wu = wload(wup[:, :, ub * 512:(ub + 1) * 512], 8, 512)
            for q4 in range(4):
                j = ub * 4 + q4
                ps = bank()
                mmgroup(ps, Wu[0], Wu[1], q4 * 128, ycs, 8)
                rl_ = tmp()
                kb.op("act", lambda e: e.activation(out=rl_[:], in_=ps[:], func=AF.Relu), reads=[ps], writes=[rl_])
                kb.op("pool", lambda e: e.tensor_tensor(out=u2[:, j, :], in0=rl_[:], in1=rl_[:], op=ALU.mult), reads=[rl_], writes=[u2])
            wsC.release(1)
        for n in range(8):
            Wd = wload(wdn[:, :, n * 128:(n + 1) * 128], 32, 128)
            ps = bank()
            mmgroup(ps, Wd[0], Wd[1], 0, u2, 32)
            kb.op("act", lambda e: e.activation(out=cvs[:, n, :], in_=ps[:], func=AF.Copy), reads=[ps], writes=[cvs])
            stat_acc(st1, ps[:], ps, n, square=True)
            wsC.release(1)
        rstd_from(st1, rstd)
        for n in range(8):
            tt_ = tmp()
            kb.op("dve", lambda e: e.scalar_tensor_tensor(out=tt_[:], in0=cvs[:, n, :], scalar=pvs[:, GMPOST, n:n + 1], in1=rstd[:], op0=ALU.mult, op1=ALU.mult),
                  reads=[cvs, pvs, rstd], writes=[tt_])
            kb.op("pool", lambda e: e.tensor_tensor(out=xs[:, n, :], in0=xs[:, n, :], in1=tt_[:], op=ALU.add), reads=[xs, tt_], writes=[xs])
        kb.dma("sp", xo[:, :, ts], xs[:], reads=[xs])
    return kb.finish()


def fm(a2d):
    tok, F = a2d.shape
    return np.ascontiguousarray(a2d.reshape(tok, F // 128, 128).transpose(2, 1, 0))


def unfm(a3d):
    p, C, tok = a3d.shape
    return np.ascontiguousarray(a3d.transpose(2, 1, 0).reshape(tok, C * p))


def wfm(w):
    K, N = w.shape
    return np.ascontiguousarray(w.reshape(K // 128, 128, N).transpose(1, 0, 2))


def colvec(v):
    return np.ascontiguousarray(v.reshape(-1, 128).T)


def la_weight(w_in_l):
    q = w_in_l[:, 0:1536]
    k = w_in_l[:, 1536:3072]
    v = w_in_l[:, 3072:4608]
    lx = w_in_l[:, 4608:5632]
    lg = w_in_l[:, 5632:6656]
    gv = w_in_l[:, 6656:7680]
    gg = w_in_l[:, 7680:8704]
    swap = np.arange(1536).reshape(24, 2, 32)[:, ::-1, :].reshape(-1)
    cols = []
    for m in (q, k):
        msw = m[:, swap]
        for j in range(12):
            cols.append(m[:, j * 128:(j + 1) * 128])
            cols.append(msw[:, j * 128:(j + 1) * 128])
    cols.append(v)
    cols.append(lx)
    cols.append(lg)
    for j in range(8):
        cols.append(gv[:, j * 128:(j + 1) * 128])
        cols.append(gg[:, j * 128:(j + 1) * 128])
    W = np.concatenate(cols, axis=1)
    assert W.shape[1] == NA
    return wfm(W)


def rope_consts():
    inv_freq = (10000.0 ** (-np.arange(0, 64, 2, dtype=np.float32) / 64)).astype(np.float32)
    c = np.zeros((128, 2), np.float32)
    for p in range(128):
        c[p, 0] = inv_freq[p % 32]
        c[p, 1] = -1.0 if (p // 32) % 2 == 0 else 1.0
    return c


_CACHE = {}


def get_prog(name):
    if name not in _CACHE:
        _CACHE[name] = {"LA": build_LA, "LB": build_LB, "LC": build_LC}[name]()
    return _CACHE[name]


def run_LA(x_all, positions, inputs, l):
    wA = la_weight(inputs["w_in"][l])
    gpre = colvec(inputs["norm_mix_pre"][l])
    cr = rope_consts()
    in_maps = []
    for c in range(8):
        b, j = divmod(c, 4)
        sl = slice(j * TOK, (j + 1) * TOK)
        in_maps.append({
            "xT": fm(x_all[b, sl]),
            "pos": np.ascontiguousarray(np.broadcast_to(positions[b, sl][None, :], (128, TOK))).astype(np.int32),
            "crope": cr, "gpre": gpre, "wA": wA,
        })
    res = run_bass_kernel_spmd(get_prog("LA"), in_maps, core_ids=list(range(8)))
    return res.results


def class_perm(a, d):
    Sx, F = a.shape
    return a.reshape(Sx // d, d, F).transpose(1, 0, 2).reshape(Sx, F)


def band_mask():
    k = np.arange(128)[:, None]
    q = np.arange(128)[None, :]
    m = np.concatenate([(k <= q), (k >= q)], axis=1)
    return m.astype(np.float32).astype(NPBF)


def blockdiag(w2):
    o = np.zeros((128, 128), np.float32)
    o[:64, :64] = w2[0]
    o[64:, 64:] = w2[1]
    return o


def run_LB(q_full, k_full, v_full, lx_full, gel_full, ug_full, inputs, l):
    msk = band_mask()
    in_maps = []
    for c in range(8):
        qa = np.zeros((64, 6, S), NPBF)
        ka = np.zeros((64, 6, S), NPBF)
        va = np.zeros((128, 6, 64, 64), NPBF)
        for b in range(B):
            for g in range(3):
                cs = slice(g * 512 + c * 64, g * 512 + (c + 1) * 64)
                bg = b * 3 + g
                qa[:, bg, :] = class_perm(q_full[b][:, cs], DIL[g]).T
                ka[:, bg, :] = class_perm(k_full[b][:, cs], DIL[g]).T
                va[:, bg, :, :] = class_perm(v_full[b][:, cs], DIL[g]).reshape(64, 128, 64).transpose(1, 0, 2)
        ch = slice(c * 128, (c + 1) * 128)
        lp = np.zeros((128, 8), np.float32)
        lp[:, 0:4] = inputs["lru_conv_w"][l][:, ch].T
        lp[:, 4] = inputs["lru_conv_b"][l][ch]
        lp[:, 5] = inputs["lru_ba"][l][ch]
        lp[:, 6] = inputs["lru_bx"][l][ch]
        lp[:, 7] = inputs["lru_lambda"][l][ch]
        wab = np.concatenate([blockdiag(inputs["lru_wa"][l][2 * c:2 * c + 2]), blockdiag(inputs["lru_wx"][l][2 * c:2 * c + 2])], axis=1)
        cw = np.zeros((128, 32), np.float32)
        cw[:, 0:31] = inputs["conv_dw_w"][l][:, ch].T
        cw[:, 31] = inputs["conv_dw_b"][l][ch]
        in_maps.append({
            "qa": qa, "ka": ka, "va": va, "cmask": msk,
            "lx": np.ascontiguousarray(lx_full[:, :, ch].transpose(2, 0, 1)).astype(np.float32),
            "gel": np.ascontiguousarray(gel_full[:, :, ch].transpose(2, 0, 1)).astype(np.float32),
            "ug": np.ascontiguousarray(ug_full[:, :, ch].transpose(2, 0, 1)).astype(np.float32),
            "lp": lp, "wab": np.ascontiguousarray(wab), "cw": cw,
        })
    res = run_bass_kernel_spmd(get_prog("LB"), in_maps, core_ids=list(range(8))).results
    o_full = np.zeros((B, S, 512), NPBF)
    yb_full = np.zeros((B, S, 1024), NPBF)
    cv_full = np.zeros((B, S, 1024), np.float32)
    for c in range(8):
        o_full[:, :, c * 64:(c + 1) * 64] = np.asarray(res[c]["oT"]).transpose(1, 2, 0)
        yb_full[:, :, c * 128:(c + 1) * 128] = np.asarray(res[c]["yb"]).transpose(1, 2, 0)
        cv_full[:, :, c * 128:(c + 1) * 128] = np.asarray(res[c]["cv"]).transpose(1, 2, 0)
    return o_full, yb_full, cv_full


def run_LC(x_all, h_full, o_full, yb_full, cv_full, inputs, l):
    pv = np.stack([colvec(inputs["conv_ln_g"][l]), colvec(inputs["conv_ln_b"][l]), colvec(inputs["norm_mix_post"][l]),
                   colvec(inputs["norm_mlp_pre"][l]), colvec(inputs["norm_mlp_post"][l])], axis=1).astype(np.float32)
    pv = np.ascontiguousarray(pv)
    wgm = wfm(np.ascontiguousarray(inputs["w_in"][l][:, 8704:11776]))
    woa = wfm(inputs["w_o_attn"][l])
    wol = wfm(inputs["w_o_lru"][l])
    woc = wfm(inputs["w_o_conv"][l])
    wout = wfm(inputs["w_out"][l])
    wup = wfm(inputs["w_mlp_up"][l])
    wdn = wfm(inputs["w_mlp_down"][l])
    in_maps = []
    for c in range(8):
        b, j = divmod(c, 4)
        sl = slice(j * TOK, (j + 1) * TOK)
        in_maps.append({
            "xT": fm(x_all[b, sl]), "hT": fm(h_full[b, sl]), "oT": fm(o_full[b, sl]), "ybT": fm(yb_full[b, sl]),
            "cvT": fm(cv_full[b, sl]), "pv": pv, "wg": wgm, "woa": woa, "wol": wol, "woc": woc, "wout": wout,
            "wup": wup, "wdn": wdn,
        })
    res = run_bass_kernel_spmd(get_prog("LC"), in_maps, core_ids=list(range(8))).results
    x_new = np.zeros((B, S, D), np.float32)
    for c in range(8):
        b, j = divmod(c, 4)
        x_new[b, j * TOK:(j + 1) * TOK] = unfm(np.asarray(res[c]["xo"]))
    return x_new


def gather_tok(res, name, dtype):
    C = np.asarray(res[0][name]).shape[1]
    out = np.zeros((B, S, C * 128), dtype)
    for c in range(8):
        b, j = divmod(c, 4)
        out[b, j * TOK:(j + 1) * TOK] = unfm(np.asarray(res[c][name]))
    return out


def kernel(**inputs):
    inputs = {k: np.asarray(v) for k, v in inputs.items()}
    x = np.ascontiguousarray(inputs["x"]).astype(np.float32)
    positions = inputs["positions"]
    for l in range(NL):
        res = run_LA(x, positions, inputs, l)
        q_full = gather_tok(res, "qT", NPBF)
        k_full = gather_tok(res, "kT", NPBF)
        v_full = gather_tok(res, "vT", NPBF)
        lx_full = gather_tok(res, "lxT", np.float32)
        gel_full = gather_tok(res, "gelT", np.float32)
        ug_full = gather_tok(res, "ugT", np.float32)
        h_full = gather_tok(res, "hT", NPBF)
        del res
        o_full, yb_full, cv_full = run_LB(q_full, k_full, v_full, lx_full, gel_full, ug_full, inputs, l)
        x = run_LC(x, h_full, o_full, yb_full, cv_full, inputs, l)
    return x.astype(np.float32)
```
